# Optimizing a Trainium2 kernel written in Bass

```python
import math
import jax, jax.numpy as jnp
from jax import lax
import numpy as np

D_MODEL = 2048
BATCH = 16
SEQ = 256
DEPTH = 2
DEC_BATCH = 4
DEC_SEQ = 2048
PAST_LEN = 512

GRID_W = 64
HEAD_DIM = 64
D_ATTN = D_MODEL // 2
NA_HEADS = D_ATTN // HEAD_DIM
NA_KH = 8
NA_KW = 16
NA_QB = 16
NA_KB = NA_QB + NA_KW
D_CONV = D_MODEL // 4
CONV_K = 31
D_SSM = D_MODEL // 4
SSM_P = 64
SSM_HEADS = D_SSM // SSM_P
SSM_GROUPS = 2
SSM_N = 128
SSM_CONV = 5
SSM_CHUNK = 128
D_XBC = D_SSM + 2 * SSM_GROUPS * SSM_N
D_IN = 3 * D_ATTN + 2 * D_CONV + D_SSM + D_XBC + 2 * SSM_HEADS
D_FF = 256 * ((8 * D_MODEL // 3 + 255) // 256)
FFN_CONV = 3
ALPHA = (2 * DEPTH) ** 0.25
BETA = (8 * DEPTH) ** -0.25
LN_EPS = 1e-5
NEG_INF = -1e30

kernel_name = "hybrid_na_conformer_ssd_dit_step"


def layer_norm(x, g, b):
    xf = x.astype(jnp.float32)
    mu = jnp.mean(xf, axis=-1, keepdims=True)
    var = jnp.mean(jnp.square(xf - mu), axis=-1, keepdims=True)
    y = (xf - mu) * lax.rsqrt(var + LN_EPS)
    return (y * g.astype(jnp.float32) + b.astype(jnp.float32)).astype(x.dtype)


def rms_norm(x, g):
    xf = x.astype(jnp.float32)
    y = xf * lax.rsqrt(jnp.mean(jnp.square(xf), axis=-1, keepdims=True) + LN_EPS)
    return y * g.astype(jnp.float32)


def dwconv(x, w, b):
    k = w.shape[0]
    y = lax.conv_general_dilated(x, w[:, None, :], (1,), [(k // 2, k // 2)],
                                 dimension_numbers=("NWC", "WIO", "NWC"),
                                 feature_group_count=x.shape[-1])
    return y + b


def modulation(cond, w, b):
    m = jax.nn.silu(cond) @ w + b
    return jnp.split(m[:, None, :], 6, axis=-1)


def context_attention(q, k, v):
    s = jnp.einsum("bqhd,bkhd->bhqk", q, k, preferred_element_type=jnp.float32) * HEAD_DIM ** -0.5
    p = jax.nn.softmax(s, axis=-1).astype(v.dtype)
    return jnp.einsum("bhqk,bkhd->bqhd", p, v)


def neighborhood_attention(q, k, v, k_ctx, v_ctx, rpb):
    b, t, h, d = q.shape
    rows = t // GRID_W
    kh = min(NA_KH, rows)
    ncb = GRID_W // NA_QB
    r = jnp.arange(rows)
    row_idx = jnp.clip(r - kh // 2, 0, rows - kh)[:, None] + jnp.arange(kh)
    cols = jnp.arange(GRID_W).reshape(ncb, NA_QB)
    col_start = jnp.clip(cols - NA_KW // 2, 0, GRID_W - NA_KW)
    col_idx = (jnp.clip(jnp.arange(ncb) * NA_QB - NA_KW // 2, 0, GRID_W - NA_KB)[:, None]
               + jnp.arange(NA_KB))
    in_win = ((col_idx[:, None, :] >= col_start[..., None])
              & (col_idx[:, None, :] < col_start[..., None] + NA_KW))
    rel_r = row_idx - r[:, None] + NA_KH - 1
    rel_c = jnp.clip(col_idx[:, None, :] - cols[..., None] + NA_KW - 1, 0, 2 * NA_KW - 2)
    bias = rpb.astype(jnp.float32)[:, rel_r[:, None, None, :, None], rel_c[None, :, :, None, :]]
    bias = jnp.where(in_win[None, None, :, :, None, :], bias, NEG_INF)
    bias = bias.reshape(h, rows, ncb, NA_QB, kh * NA_KB)
    n_lat = kh * NA_KB
    ridx = row_idx[:, None, :, None]
    cidx = col_idx[None, :, None, :]
    k_blk = k.reshape(b, rows, GRID_W, h, d)[:, ridx, cidx].reshape(b, rows, ncb, n_lat, h, d)
    v_blk = v.reshape(b, rows, GRID_W, h, d)[:, ridx, cidx].reshape(b, rows, ncb, n_lat, h, d)
    qb = q.reshape(b, rows, ncb, NA_QB, h, d)
    scale = HEAD_DIM ** -0.5
    s_lat = jnp.einsum("brjqhd,brjkhd->bhrjqk", qb, k_blk, preferred_element_type=jnp.float32) * scale + bias
    s_ctx = jnp.einsum("brjqhd,bmhd->bhrjqm", qb, k_ctx, preferred_element_type=jnp.float32) * scale
    p = jax.nn.softmax(jnp.concatenate([s_lat, s_ctx], axis=-1), axis=-1).astype(v.dtype)
    o = (jnp.einsum("bhrjqk,brjkhd->brjqhd", p[..., :n_lat], v_blk)
         + jnp.einsum("bhrjqm,bmhd->brjqhd", p[..., n_lat:], v_ctx))
    return o.reshape(b, t, h * d)


def conformer_conv(u, g, w, bconv, ln_g, ln_b):
    hcur = dwconv(u * jax.nn.sigmoid(g), w, bconv)
    return jax.nn.silu(layer_norm(hcur, ln_g, ln_b))


def ssd_scan(x, dt, a, bm, cm, h0):
    b, t, h, p = x.shape
    n = bm.shape[-1]
    q = SSM_CHUNK
    nc = t // q
    x = x.reshape(b, nc, q, h, p)
    dt = dt.reshape(b, nc, q, h)
    bm = bm.reshape(b, nc, q, h, n)
    cm = cm.reshape(b, nc, q, h, n)
    cs = jnp.cumsum(dt * a, axis=2)
    tril = jnp.tril(jnp.ones((q, q), dtype=bool))[None, None, :, :, None]
    seg = jnp.exp(jnp.where(tril, cs[:, :, :, None, :] - cs[:, :, None, :, :], NEG_INF))
    xdt = x * dt[..., None]
    scores = jnp.einsum("bcihn,bcjhn->bcijh", cm, bm) * seg
    y_diag = jnp.einsum("bcijh,bcjhp->bcihp", scores, xdt)
    decay_end = jnp.exp(cs[:, :, -1:, :] - cs)
    states = jnp.einsum("bcjhn,bcjhp->bchpn", bm * decay_end[..., None], xdt)
    chunk_decay = jnp.exp(cs[:, :, -1, :])

    def step(hc, inp):
        st, dec = inp
        return hc * dec[:, :, None, None] + st, hc

    h_last, h_start = lax.scan(step, h0, (jnp.moveaxis(states, 1, 0), jnp.moveaxis(chunk_decay, 1, 0)))
    h_start = jnp.moveaxis(h_start, 0, 1)
    y_off = jnp.einsum("bcihn,bchpn->bcihp", cm * jnp.exp(cs)[..., None], h_start)
    return (y_diag + y_off).reshape(b, t, h, p), h_last


def ssd_mixer(z, xbc, dt_raw, conv_w, conv_b, a_log, dt_bias, d_skip, norm_g, h0):
    b, t, _ = z.shape
    f32 = jnp.float32
    xbc = jax.nn.silu(dwconv(xbc, conv_w, conv_b)).astype(f32)
    xs, bm, cm = jnp.split(xbc, [D_SSM, D_SSM + SSM_GROUPS * SSM_N], axis=-1)
    xs = xs.reshape(b, t, SSM_HEADS, SSM_P)
    rep = SSM_HEADS // SSM_GROUPS
    bm = jnp.repeat(bm.reshape(b, t, SSM_GROUPS, SSM_N), rep, axis=2)
    cm = jnp.repeat(cm.reshape(b, t, SSM_GROUPS, SSM_N), rep, axis=2)
    a = -jnp.exp(a_log.astype(f32))
    dt = jax.nn.softplus(dt_raw.astype(f32).reshape(b, t, 2, SSM_HEADS) + dt_bias.astype(f32))
    h0 = h0.astype(f32)
    y_f, h_f = ssd_scan(xs, dt[:, :, 0], a[0], bm, cm, h0[:, 0])
    flip = lambda u: jnp.flip(u, axis=1)
    y_b, h_b = ssd_scan(flip(xs), flip(dt[:, :, 1]), a[1], flip(bm), flip(cm), h0[:, 1])
    y = y_f + flip(y_b) + d_skip.astype(f32)[:, None] * xs
    y = y.reshape(b, t, D_SSM) * jax.nn.silu(z.astype(f32))
    return rms_norm(y, norm_g).astype(z.dtype), jnp.stack([h_f, h_b], axis=1)


def conv_ffn(hcur, w_up, cw, cb, w_down):
    u = dwconv(hcur @ w_up, cw, cb)
    a, g = jnp.split(u, 2, axis=-1)
    return (a * jax.nn.silu(g)) @ w_down


def trunk_layer(x, cond, lp, ctx=None):
    b, t, _ = x.shape
    sh1, sc1, g1, sh2, sc2, g2 = modulation(cond, lp["w_mod"], lp["b_mod"])
    hcur = x * (1 + sc1) + sh1
    cuts, acc = [], 0
    for s in (D_ATTN, D_ATTN, D_ATTN, D_CONV, D_CONV, D_SSM, D_XBC):
        acc += s
        cuts.append(acc)
    q, k, v, u, gt, z, xbc, dt = jnp.split(hcur @ lp["w_in"], cuts, axis=-1)
    q = q.reshape(b, t, NA_HEADS, HEAD_DIM)
    k = k.reshape(b, t, NA_HEADS, HEAD_DIM)
    v = v.reshape(b, t, NA_HEADS, HEAD_DIM)
    if ctx is None:
        o_a = context_attention(q, k, v).reshape(b, t, D_ATTN)
        h0 = jnp.zeros((b, 2, SSM_HEADS, SSM_P, SSM_N), jnp.float32)
    else:
        k_ctx, v_ctx, h0 = ctx
        o_a = neighborhood_attention(q, k, v, k_ctx, v_ctx, lp["rpb"])
    o_b = conformer_conv(u, gt, lp["conv_w"], lp["conv_b"], lp["conv_ln_g"], lp["conv_ln_b"])
    o_c, h_state = ssd_mixer(z, xbc, dt, lp["ssm_conv_w"], lp["ssm_conv_b"], lp["ssm_a_log"],
                             lp["ssm_dt_bias"], lp["ssm_d"], lp["ssm_norm_g"], h0)
    mix = jnp.concatenate([o_a, o_b, o_c], axis=-1) @ lp["w_out"]
    x = layer_norm(ALPHA * x + g1 * mix, lp["ln1_g"], lp["ln1_b"])
    hcur = x * (1 + sc2) + sh2
    ff = conv_ffn(hcur, lp["w_up"], lp["ffn_conv_w"], lp["ffn_conv_b"], lp["w_down"])
    x = layer_norm(ALPHA * x + g2 * ff, lp["ln2_g"], lp["ln2_b"])
    return x, k, v, h_state


def setup_inputs(seed: int = 0) -> dict:
    key = jax.random.key(seed)
    ks = jax.random.split(key, 32)
    f32 = jnp.float32
    L = DEPTH
    nrm = lambda kk, shape, s: jax.random.normal(kk, shape, f32) * s
    dt0 = jnp.exp(jax.random.uniform(ks[20], (L, 2, SSM_HEADS), f32, math.log(1e-3), math.log(1e-1)))
    return {
        "x_prompt": nrm(ks[0], (BATCH, SEQ, D_MODEL), 1.0),
        "x_sample": nrm(ks[1], (DEC_BATCH, DEC_SEQ, D_MODEL), 1.0),
        "cache_k": nrm(ks[2], (DEC_BATCH, DEPTH, PAST_LEN, NA_HEADS, HEAD_DIM), 1.0),
        "cache_v": nrm(ks[3], (DEC_BATCH, DEPTH, PAST_LEN, NA_HEADS, HEAD_DIM), 1.0),
        "state_ssm": nrm(ks[4], (DEC_BATCH, DEPTH, 2, SSM_HEADS, SSM_P, SSM_N), 0.5),
        "c": nrm(ks[5], (DEC_BATCH, D_MODEL), 1.0),
        "c_ctx": nrm(ks[6], (D_MODEL,), 1.0),
        "w_mod": nrm(ks[7], (L, D_MODEL, 6 * D_MODEL), 0.5 * D_MODEL ** -0.5),
        "b_mod": nrm(ks[8], (L, 6 * D_MODEL), 0.02),
        "w_in": nrm(ks[9], (L, D_MODEL, D_IN), D_MODEL ** -0.5),
        "rpb": nrm(ks[10], (L, NA_HEADS, 2 * NA_KH - 1, 2 * NA_KW - 1), 0.1),
        "conv_w": nrm(ks[11], (L, CONV_K, D_CONV), CONV_K ** -0.5),
        "conv_b": nrm(ks[12], (L, D_CONV), 0.01),
        "conv_ln_g": 1.0 + nrm(ks[13], (L, D_CONV), 0.02),
        "conv_ln_b": nrm(ks[14], (L, D_CONV), 0.02),
        "ssm_conv_w": nrm(ks[15], (L, SSM_CONV, D_XBC), SSM_CONV ** -0.5),
        "ssm_conv_b": nrm(ks[16], (L, D_XBC), 0.01),
        "ssm_a_log": jnp.log(jax.random.uniform(ks[17], (L, 2, SSM_HEADS), f32, 1.0, 16.0)),
        "ssm_dt_bias": dt0 + jnp.log(-jnp.expm1(-dt0)),
        "ssm_d": 1.0 + nrm(ks[18], (L, SSM_HEADS), 0.1),
        "ssm_norm_g": 1.0 + nrm(ks[19], (L, D_SSM), 0.02),
        "w_out": nrm(ks[21], (L, D_MODEL, D_MODEL), BETA * D_MODEL ** -0.5),
        "ln1_g": 1.0 + nrm(ks[22], (L, D_MODEL), 0.02),
        "ln1_b": nrm(ks[23], (L, D_MODEL), 0.02),
        "w_up": nrm(ks[24], (L, D_MODEL, 2 * D_FF), D_MODEL ** -0.5),
        "ffn_conv_w": nrm(ks[25], (L, FFN_CONV, 2 * D_FF), FFN_CONV ** -0.5),
        "ffn_conv_b": nrm(ks[26], (L, 2 * D_FF), 0.01),
        "w_down": nrm(ks[27], (L, D_FF, D_MODEL), BETA * D_FF ** -0.5),
        "ln2_g": 1.0 + nrm(ks[28], (L, D_MODEL), 0.02),
        "ln2_b": nrm(ks[29], (L, D_MODEL), 0.02),
    }


def reference(x_prompt, x_sample, cache_k, cache_v, state_ssm, c, c_ctx, w_mod, b_mod, w_in, rpb,
              conv_w, conv_b, conv_ln_g, conv_ln_b, ssm_conv_w, ssm_conv_b, ssm_a_log, ssm_dt_bias,
              ssm_d, ssm_norm_g, w_out, ln1_g, ln1_b, w_up, ffn_conv_w, ffn_conv_b, w_down, ln2_g, ln2_b):
    y_prompt = x_prompt
    y_sample = x_sample
    ks, vs, hs = [], [], []
    for l in range(DEPTH):
        lp = dict(w_mod=w_mod[l], b_mod=b_mod[l], w_in=w_in[l], rpb=rpb[l], conv_w=conv_w[l],
                  conv_b=conv_b[l], conv_ln_g=conv_ln_g[l], conv_ln_b=conv_ln_b[l],
                  ssm_conv_w=ssm_conv_w[l], ssm_conv_b=ssm_conv_b[l], ssm_a_log=ssm_a_log[l],
                  ssm_dt_bias=ssm_dt_bias[l], ssm_d=ssm_d[l], ssm_norm_g=ssm_norm_g[l],
                  w_out=w_out[l], ln1_g=ln1_g[l], ln1_b=ln1_b[l], w_up=w_up[l],
                  ffn_conv_w=ffn_conv_w[l], ffn_conv_b=ffn_conv_b[l], w_down=w_down[l],
                  ln2_g=ln2_g[l], ln2_b=ln2_b[l])
        y_prompt, k_l, v_l, h_l = trunk_layer(y_prompt, c_ctx[None, :], lp)
        ks.append(k_l)
        vs.append(v_l)
        hs.append(h_l)
        y_sample, _, _, _ = trunk_layer(y_sample, c, lp, (cache_k[:, l], cache_v[:, l], state_ssm[:, l]))
    new_cache_k = jnp.stack(ks, axis=1)
    new_cache_v = jnp.stack(vs, axis=1)
    new_state_ssm = jnp.stack(hs, axis=1)
    return (y_prompt, y_sample, new_cache_k, new_cache_v, new_state_ssm)
```

```python
import numpy as np
from contextlib import ExitStack
import concourse.bass as bass
import concourse.mybir as mybir
from concourse.bass_utils import run_bass_kernel_spmd

F32 = mybir.dt.float32
BF16 = mybir.dt.bfloat16
AF = mybir.ActivationFunctionType
ALU = mybir.AluOpType

N_DMA_SEMS = 24
D = 2048
DIN = 5648
DFF = 5632
ALPHA = 4 ** 0.25
EPS = 1e-5
NEG = -30000.0


class T:
    __slots__ = ("ap", "w", "r", "psem", "psum")

    def __init__(self, ap):
        self.ap = ap
        self.w = None
        self.r = []
        self.psum = False


class KB:
    def __init__(self):
        self.nc = bass.Bass("TRN2", target_bir_lowering=False)
        nc = self.nc
        self.es = ExitStack()
        self.eng = {"pe": nc.tensor, "act": nc.scalar, "dve": nc.vector, "pool": nc.gpsimd, "sp": nc.sync}
        self.sem, self.cnt = {}, {}
        for e in self.eng:
            self.sem[e] = self.es.enter_context(nc.semaphore("s_" + e))
            self.cnt[e] = 0
        for i in range(N_DMA_SEMS):
            k = "d%d" % i
            self.sem[k] = self.es.enter_context(nc.semaphore("s_" + k))
            self.cnt[k] = 0
        for i in range(3):
            k = "q%d" % i
            self.sem[k] = self.es.enter_context(nc.semaphore("s_" + k))
            self.cnt[k] = 0
        self.dma_rr = 0
        self.seen = {e: {} for e in self.eng}
        self.uid = 0
        self.dq = 0
        self.stack = [self.es]

    def sb(self, shape, dt=F32):
        self.uid += 1
        return T(self.stack[-1].enter_context(self.nc.sbuf_tensor("sb%d" % self.uid, list(shape), dt)))

    def ps(self, shape, dt=F32):
        self.uid += 1
        t = T(self.es.enter_context(self.nc.psum_tensor("ps%d" % self.uid, list(shape), dt)))
        t.psum = True
        return t

    def barrier(self):
        toks = [(k, v) for k, v in self.cnt.items() if v > 0]
        for e in self.eng:
            for tok in toks:
                self._wait(e, tok)

    def push(self):
        es = ExitStack()
        self.stack.append(es)
        return es

    def pop(self):
        self.barrier()
        self.stack.pop().close()

    def dram(self, name, shape, dt=F32, kind="Internal"):
        return self.nc.dram_tensor(name, list(shape), dt, kind=kind).ap()

    def _wait(self, e, tok):
        if tok is None:
            return
        if isinstance(tok, list):
            for t_ in tok:
                self._wait(e, t_)
            return
        k, v = tok
        if k == e and e == "pe":
            return
        if not k.startswith("q") and self.seen[e].get(k, 0) >= v:
            return
        self.eng[e].wait_ge(self.sem[k], v)
        self.seen[e][k] = v

    def _deps(self, e, reads, writes):
        need = {}

        def add(tok):
            if tok is None:
                return
            if isinstance(tok, list):
                for t_ in tok:
                    add(t_)
                return
            k, v = tok
            if need.get(k, 0) < v:
                need[k] = v
        for t in reads:
            add(t.w)
            if t.psum:
                for tok in t.r:
                    if tok[0] != e:
                        add(tok)
        for t in writes:
            add(t.w)
            for tok in t.r:
                add(tok)
        for k, v in need.items():
            self._wait(e, (k, v))

    def pdma(self, wb, out, in_):
        e = "pool"
        self._deps(e, [], [wb])
        ins_c = self.eng[e].sem_clear(self.sem[wb.psem])
        self.cnt[e] += 1
        ins_c.then_inc(self.sem[e], 1)
        ctok = (e, self.cnt[e])
        ins = self.eng[e].dma_start(out=out, in_=in_)
        ins.then_inc(self.sem[wb.psem], 16)
        wb.w = [ctok, (wb.psem, 16)]
        wb.r = []
        return ins

    def _mark(self, tok, reads, writes):
        for t in reads:
            t.r.append(tok)
            if len(t.r) > 16:
                d = {}
                for k, v in t.r:
                    d[k] = max(d.get(k, 0), v)
                t.r = list(d.items())
        for t in writes:
            t.w = tok
            t.r = []

    def op(self, e, fn, reads=(), writes=()):
        self._deps(e, reads, writes)
        ins = fn(self.eng[e])
        self.cnt[e] += 1
        ins.then_inc(self.sem[e], 1)
        self._mark((e, self.cnt[e]), reads, writes)
        return ins

    def dma(self, e, out, in_, reads=(), writes=(), **kw):
        k = "d%d" % self.dma_rr
        self.dma_rr = (self.dma_rr + 1) % N_DMA_SEMS
        if self.cnt[k] > 0:
            self._wait(e, (k, self.cnt[k]))
        self._deps(e, reads, writes)
        ins = self.eng[e].dma_start(out=out, in_=in_, **kw)
        self.cnt[k] += 16
        ins.then_inc(self.sem[k], 16)
        self._mark((k, self.cnt[k]), reads, writes)
        return ins

    def finish(self):
        for i in range(N_DMA_SEMS):
            k = "d%d" % i
            if self.cnt[k]:
                self._wait("sp", (k, self.cnt[k]))
        for e in ("pe", "act", "dve", "pool"):
            if self.cnt[e]:
                self._wait("sp", (e, self.cnt[e]))
        self.es.close()


class StopBuild(Exception):
    pass


STOP = None


def ckpt(name):
    if STOP == name:
        raise StopBuild(name)


def build():
    kb = KB()
    nc = kb.nc
    EI = "ExternalInput"
    EO = "ExternalOutput"
    xs_d = kb.dram("xs", [2048, D], kind=EI)
    xp_d = kb.dram("xp", [512, D], kind=EI)
    ck_d = kb.dram("ck", [2, 512, 1024], kind=EI)
    cv_d = kb.dram("cv", [2, 512, 1024], kind=EI)
    st_d = kb.dram("st", [2, 2, 8, 64, 128], kind=EI)
    cvec_d = kb.dram("cvec", [2, D], kind=EI)
    wmod_d = kb.dram("w_mod", [2, D, 6 * D], kind=EI)
    bmod_d = kb.dram("b_mod", [2, 6 * D], kind=EI)
    win_d = kb.dram("w_in", [2, D, DIN], kind=EI)
    wout_d = kb.dram("w_out", [2, D, D], kind=EI)
    wup_d = kb.dram("w_up", [2, D, 2 * DFF], kind=EI)
    wdn_d = kb.dram("w_down", [2, DFF, D], kind=EI)
    ww_d = kb.dram("ww", [2, 16, 128, 9 * 128], kind=EI)
    colp_d = kb.dram("colp", [2, 128, 540], kind=EI)
    rowp_d = kb.dram("rowp", [2, 10272], kind=EI)
    ys_d = kb.dram("ys", [2048, D], kind=EO)
    yp_d = kb.dram("yp", [512, D], kind=EO)
    nk_d = kb.dram("nk", [2, 2, 256, 1024], kind=EO)
    nv_d = kb.dram("nv", [2, 2, 256, 1024], kind=EO)
    ns_d = kb.dram("ns", [2, 2, 2, 8, 64, 128], kind=EO)
    modd = kb.dram("modd", [2, 2, 6 * D])
    x1s_d = kb.dram("x1s", [2048, D])
    x1p_d = kb.dram("x1p", [512, D])
    x2s_d = kb.dram("x2s", [2048, D])
    x2p_d = kb.dram("x2p", [512, D])
    ccs_d = kb.dram("ccs", [128, 16, 2048], BF16)
    hs_d = kb.dram("hsd", [16, 128, 2, 512], BF16)
    dT = {}
    for nm, ap in (("modd", modd), ("x1s", x1s_d), ("x1p", x1p_d), ("x2s", x2s_d), ("x2p", x2p_d),
                   ("ys", ys_d), ("yp", yp_d), ("nk", nk_d), ("nv", nv_d), ("ns", ns_d), ("ccs", ccs_d), ("hs", hs_d)):
        dT[nm] = T(ap)

    def dump(name, t, ap, dt=F32):
        if STOP is None:
            return
        dd = kb.dram("dbg_" + name, list(ap.shape), dt, kind=EO)
        kb.dma("sp", dd, ap, reads=[t], writes=[T(dd)])

    ident = kb.sb([128, 128], F32)
    identb = kb.sb([128, 128], BF16)
    onesb = kb.sb([128, 128], BF16)
    onesf = kb.sb([128, 128], F32)
    LT = kb.sb([128, 128], F32)
    UT = kb.sb([128, 128], F32)
    kb.op("pool", lambda e: e.memset(ident.ap[:], 0.0), writes=[ident])
    kb.op("pool", lambda e: e.affine_select(out=ident.ap[:], in_=ident.ap[:], pattern=[[-1, 128]],
                                            compare_op=ALU.not_equal, fill=1.0, base=0, channel_multiplier=1),
          reads=[ident], writes=[ident])
    kb.op("dve", lambda e: e.tensor_copy(out=identb.ap[:], in_=ident.ap[:]), reads=[ident], writes=[identb])
    kb.op("pool", lambda e: e.memset(onesb.ap[:], 1.0), writes=[onesb])
    kb.op("pool", lambda e: e.memset(onesf.ap[:], 1.0), writes=[onesf])
    kb.op("pool", lambda e: e.memset(LT.ap[:], 1.0), writes=[LT])
    kb.op("pool", lambda e: e.memset(UT.ap[:], 1.0), writes=[UT])
    kb.op("pool", lambda e: e.affine_select(out=LT.ap[:], in_=LT.ap[:], pattern=[[1, 128]], compare_op=ALU.is_ge,
                                            fill=0.0, base=0, channel_multiplier=-1), reads=[LT], writes=[LT])
    kb.op("pool", lambda e: e.affine_select(out=UT.ap[:], in_=UT.ap[:], pattern=[[-1, 128]], compare_op=ALU.is_ge,
                                            fill=0.0, base=0, channel_multiplier=1), reads=[UT], writes=[UT])

    banks = [kb.ps([128, 512], F32) for _ in range(6)]
    kb_psb = [kb.ps([128, 512], BF16) for _ in range(2)]
    st8 = {"i": 0}
    pinned = set()

    def pbank():
        while True:
            b = banks[st8["i"] % 6]
            st8["i"] += 1
            if id(b) not in pinned:
                return b

    bigA = kb.sb([128, 32768], BF16)
    pools = {"w": [], "s": []}
    wst = {"i": 0}

    def alloc_pools(nw, ns_):
        pools["w"] = [kb.sb([128, 4096], BF16) for _ in range(nw)]
        pools["s"] = [kb.sb([128, 2048], F32) for _ in range(ns_)]

    def wbuf():
        b = pools["w"][wst["i"] % len(pools["w"])]
        wst["i"] += 1
        return b

    def dq():
        return "act"

    colp = kb.sb([128, 540], F32)
    modc = kb.sb([128, 4, 16], F32)
    rowS = kb.sb([128, 1568], F32)
    dexp = kb.sb([128, 512], F32)
    xres = kb.sb([128, D], F32)
    stats = kb.sb([128, 4, 6], F32)
    mv = kb.sb([128, 2], F32)
    rstd = kb.sb([128, 2], F32)
    rr = {"i": 0}

    def rot(lst):
        rr["i"] += 1
        return lst[rr["i"] % len(lst)]

    stg_rr = {"i": 0}
    defer = {"l": None}
    CAST_ENG = ("act", "act", "dve")

    def stage_cast(wb, dst, src):
        a, b = dst.shape[1], dst.shape[2]
        stg = pools["s"][stg_rr["i"] % len(pools["s"])]
        ce = CAST_ENG[stg_rr["i"] % len(CAST_ENG)]
        stg_rr["i"] += 1
        sv = stg.ap[:, 0:a * b].rearrange("p (a b) -> p a b", a=a)
        kb.dma("sp", sv, src, writes=[stg])

        def do_cast():
            if ce == "act":
                kb.op("act", lambda e: e.activation(out=dst, in_=sv, func=AF.Identity), reads=[stg], writes=[wb])
            else:
                kb.op(ce, lambda e: e.tensor_copy(out=dst, in_=sv), reads=[stg], writes=[wb])
        if defer["l"] is None:
            do_cast()
        else:
            defer["l"].append(do_cast)

    def pipelined(n, load, compute, depth=1):
        loaded = {}
        for i in range(min(depth, n)):
            loaded[i] = load(i)
        for i in range(n):
            if i + depth < n:
                loaded[i + depth] = load(i + depth)
            compute(i, loaded.pop(i))

    def pipelined3(n, load, compute):
        res, pend = {}, {}

        def A(i):
            defer["l"] = []
            res[i] = load(i)
            pend[i] = defer["l"]
            defer["l"] = None

        def B(i):
            for c in pend.pop(i):
                c()
        A(0)
        if n > 1:
            A(1)
        B(0)
        for i in range(n):
            if i + 2 < n:
                A(i + 2)
            if i + 1 < n:
                B(i + 1)
            compute(i, res.pop(i))

    def load_w(src2d, kc, ncols):
        wb = wbuf()
        view = wb.ap[:, 0:kc * ncols].rearrange("p (c n) -> p c n", c=kc)
        srcv = src2d.rearrange("(c p) n -> p c n", p=128)
        step = max(1, 2048 // ncols)
        for c0 in range(0, kc, step):
            c1 = min(kc, c0 + step)
            stage_cast(wb, view[:, c0:c1, :], srcv[:, c0:c1, :])
        return wb, view

    def build_hT(dst, dview, xsrc, src_T, ntok, col0, sc_i, sh_i):
        xt = xres
        kb.dma(dq(), xt.ap[0:ntok, :], xsrc, reads=[src_T], writes=[xt])
        for q in range(4):
            pb = pbank()
            for kk in range(4):
                k = q * 4 + kk
                kb.op("pe", lambda e: e.transpose(pb.ap[:, kk * 128:kk * 128 + ntok], xt.ap[0:ntok, k * 128:(k + 1) * 128],
                                                  ident.ap[0:ntok, 0:ntok]), reads=[xt, ident], writes=[pb])
            for kk in range(4):
                k = q * 4 + kk
                kb.op("act", lambda e: e.activation(out=dview[:, k, col0:col0 + ntok], in_=pb.ap[:, kk * 128:kk * 128 + ntok],
                                                    func=AF.Identity, scale=modc.ap[:, sc_i, k:k + 1], bias=modc.ap[:, sh_i, k:k + 1]),
                      reads=[pb, modc], writes=[dst])

    def layer_norm_rows(src, sv, width, gtile, gview, btile, bview, out):
        nch = (width + 511) // 512
        for c in range(nch):
            kb.op("dve", lambda e: e.bn_stats(out=stats.ap[:, c, :], in_=sv[:, c * 512:min(width, (c + 1) * 512)]),
                  reads=[src], writes=[stats])
        kb.op("dve", lambda e: e.bn_aggr(out=mv.ap[:, :], in_=stats.ap[:, 0:nch, :]), reads=[stats], writes=[mv])
        kb.op("dve", lambda e: e.tensor_scalar_add(out=rstd.ap[:, 0:1], in0=mv.ap[:, 1:2], scalar1=EPS), reads=[mv], writes=[rstd])
        kb.op("act", lambda e: e.activation(out=rstd.ap[:, 0:1], in_=rstd.ap[:, 0:1], func=AF.Sqrt), reads=[rstd], writes=[rstd])
        kb.op("dve", lambda e: e.reciprocal(out=rstd.ap[:, 0:1], in_=rstd.ap[:, 0:1]), reads=[rstd], writes=[rstd])
        kb.op("dve", lambda e: e.scalar_tensor_tensor(out=rstd.ap[:, 1:2], in0=mv.ap[:, 0:1], scalar=-1.0, in1=rstd.ap[:, 0:1],
                                                      op0=ALU.mult, op1=ALU.mult), reads=[mv, rstd], writes=[rstd])
        kb.op("act", lambda e: e.activation(out=sv, in_=sv, func=AF.Identity,
                                            scale=rstd.ap[:, 0:1], bias=rstd.ap[:, 1:2]), reads=[src, rstd], writes=[src])
        kb.op("dve", lambda e: e.tensor_tensor(out=sv, in0=sv, in1=gview, op=ALU.mult),
              reads=[src, gtile], writes=[src])
        kb.op("pool", lambda e: e.tensor_tensor(out=out.ap[:, 0:width], in0=sv, in1=bview, op=ALU.add),
              reads=[src, btile], writes=[out])

    kb.push()
    vbuf = xres
    cT = kb.sb([128, 16, 2], F32)
    kb.dma("sp", vbuf.ap[0:2, :], cvec_d[:, :], writes=[vbuf])
    kb.op("act", lambda e: e.activation(out=vbuf.ap[0:2, :], in_=vbuf.ap[0:2, :], func=AF.Silu), reads=[vbuf], writes=[vbuf])
    pb = pbank()
    for k in range(16):
        kb.op("pe", lambda e: e.transpose(pb.ap[:, 2 * k:2 * k + 2], vbuf.ap[0:2, k * 128:(k + 1) * 128], ident.ap[0:2, 0:2]),
              reads=[vbuf, ident], writes=[pb])
    kb.op("dve", lambda e: e.tensor_copy(out=cT.ap[:].rearrange("p k c -> p (k c)"), in_=pb.ap[:, 0:32]), reads=[pb], writes=[cT])
    mrow = kb.sb([2, 512], F32)
    brow = kb.sb([2, 512], F32)
    bigAf = bigA.ap[:, :].bitcast(F32)
    wmT = [T(bigAf[:, 0:8192]), T(bigAf[:, 8192:16384])]
    for l in range(2):
        for g in range(24):
            wm = wmT[g % 2]
            wmv = wm.ap.rearrange("p (c n) -> p c n", c=16)
            for hf in range(2):
                kb.dma("act", wmv[:, hf * 8:(hf + 1) * 8, :],
                       wmod_d[l, hf * 1024:(hf + 1) * 1024, g * 512:(g + 1) * 512].rearrange("(c p) n -> p c n", p=128), writes=[wm])
            kb.dma("sp", brow.ap[:], bmod_d[l:l + 1, g * 512:(g + 1) * 512].partition_broadcast(2), writes=[brow])
            pb = pbank()
            for k in range(16):
                kb.op("pe", lambda e: e.matmul(pb.ap[0:2, :], lhsT=cT.ap[:, k, :], rhs=wmv[:, k, :], start=(k == 0), stop=(k == 15)),
                      reads=[cT, wm], writes=[pb])
            kb.op("dve", lambda e: e.tensor_tensor(out=mrow.ap[:], in0=pb.ap[0:2, :], in1=brow.ap[:], op=ALU.add),
                  reads=[pb, brow], writes=[mrow])
            kb.dma("sp", modd[l, :, g * 512:(g + 1) * 512], mrow.ap[:], reads=[mrow], writes=[dT["modd"]])
    kb.pop()

    def run_job(l, xin_d, xin_T, x1_d, x1_T, xout_d, xout_T, Tn, nseq, L, crow, sample):
        ntile = Tn // 128
        hT_v = bigA.ap[:, 0:16 * Tn].rearrange("p (k t) -> p k t", k=16)
        kb.dma("sp", colp.ap[:], colp_d[l, :, :], writes=[colp])
        for i, off in enumerate((D, 0, 4 * D, 3 * D)):
            kb.dma("sp", modc.ap[:, i, :], modd[l, crow, off:off + D].rearrange("(c p) -> p c", p=128),
                   reads=[dT["modd"]], writes=[modc], allow_slow_non_contiguous=True)
        for i in (0, 2):
            kb.op("dve", lambda e: e.tensor_scalar_add(out=modc.ap[:, i, :], in0=modc.ap[:, i, :], scalar1=1.0), reads=[modc], writes=[modc])
        kb.dma("sp", rowS.ap[:], rowp_d[l:l + 1, 8192:9760].partition_broadcast(128), writes=[rowS])
        kb.dma("sp", dexp.ap[:], rowp_d[l:l + 1, 9760:10272].partition_broadcast(128), writes=[dexp])
        kb.op("act", lambda e: e.activation(out=rowS.ap[:, 1536:1552], in_=rowS.ap[:, 1536:1552], func=AF.Exp), reads=[rowS], writes=[rowS])
        kb.op("dve", lambda e: e.tensor_scalar_mul(out=rowS.ap[:, 1536:1552], in0=rowS.ap[:, 1536:1552], scalar1=-1.0), reads=[rowS], writes=[rowS])

        for tt in range(ntile):
            build_hT(bigA, hT_v, xin_d[tt * 128:(tt + 1) * 128, :], xin_T, 128, tt * 128, 0, 1)

        wview_t = [None]

        def proj_fm(wview, c0, t0, n, pb):
            for k in range(16):
                kb.op("pe", lambda e: e.matmul(pb.ap[:, 0:n], lhsT=wview[:, k, c0:c0 + 128], rhs=hT_v[:, k, t0:t0 + n],
                                               start=(k == 0), stop=(k == 15)), reads=[bigA, wview_t[0]], writes=[pb])

        def proj_tm(wview, c0, ncols, tt, pb):
            for k in range(16):
                kb.op("pe", lambda e: e.matmul(pb.ap[:, 0:ncols], lhsT=hT_v[:, k, tt * 128:(tt + 1) * 128], rhs=wview[:, k, c0:c0 + ncols],
                                               start=(k == 0), stop=(k == 15)), reads=[bigA, wview_t[0]], writes=[pb])
        ckpt("A")

        kb.push()
        alloc_pools(6, 3)
        qT = kb.sb([128, Tn], BF16)
        kT = kb.sb([128, Tn], BF16)
        Vt = kb.sb([128, ntile, 128], BF16)
        ccst = kb.sb([128, Tn], BF16)
        rden = [kb.sb([128, 256], F32) for _ in range(2)]
        if sample:
            kcT = kb.sb([128, 512], BF16)
            Vc = kb.sb([128, 4, 128], BF16)
            ktmp = kb.sb([128, 4, 128], F32)
            wwt = kb.sb([128, 2, 9 * 128], BF16)
            Eb = [kb.sb([128, 1152], BF16) for _ in range(2)]
            Sb = [kb.sb([128, 640], F32) for _ in range(2)]
        else:
            Eb = [kb.sb([128, 512], BF16) for _ in range(2)]
            kvout = [kb.sb([128, 128], F32) for _ in range(2)]
        def att_load(hp):
            return (load_w(win_d[l, :, hp * 128:(hp + 1) * 128], 16, 128),
                    load_w(win_d[l, :, 1024 + hp * 128:1024 + (hp + 1) * 128], 16, 128),
                    load_w(win_d[l, :, 2048 + hp * 128:2048 + (hp + 1) * 128], 16, 128))

        def att_compute(hp, ws):
            (wq_t, wq), (wk_t, wk), (wv_t, wv) = ws
            for t0 in range(0, Tn, 512):
                pb = pbank()
                wview_t[0] = wq_t
                proj_fm(wq, 0, t0, 512, pb)
                kb.op("act", lambda e: e.activation(out=qT.ap[:, t0:t0 + 512], in_=pb.ap[:, :], func=AF.Identity, scale=0.125),
                      reads=[pb], writes=[qT])
                pb = pbank()
                wview_t[0] = wk_t
                proj_fm(wk, 0, t0, 512, pb)
                kb.op("dve", lambda e: e.tensor_copy(out=kT.ap[:, t0:t0 + 512], in_=pb.ap[:, :]), reads=[pb], writes=[kT])
            for tt in range(ntile):
                pb = pbank()
                wview_t[0] = wv_t
                proj_tm(wv, 0, 128, tt, pb)
                if sample:
                    kb.op("act", lambda e: e.activation(out=Vt.ap[:, tt, :], in_=pb.ap[:, 0:128], func=AF.Identity), reads=[pb], writes=[Vt])
                else:
                    s, tl = divmod(tt * 128, L)
                    ko = rot(kvout)
                    kb.op("dve", lambda e: e.tensor_copy(out=ko.ap[:], in_=pb.ap[:, 0:128]), reads=[pb], writes=[ko])
                    kb.op("act", lambda e: e.activation(out=Vt.ap[:, tt, :], in_=ko.ap[:], func=AF.Identity), reads=[ko], writes=[Vt])
                    kb.dma("sp", nv_d[s, l, tl:tl + 128, hp * 128:(hp + 1) * 128], ko.ap[:], reads=[ko], writes=[dT["nv"]])
                    pb2 = pbank()
                    wview_t[0] = wk_t
                    proj_tm(wk, 0, 128, tt, pb2)
                    ko2 = rot(kvout)
                    kb.op("dve", lambda e: e.tensor_copy(out=ko2.ap[:], in_=pb2.ap[:, 0:128]), reads=[pb2], writes=[ko2])
                    kb.dma("sp", nk_d[s, l, tl:tl + 128, hp * 128:(hp + 1) * 128], ko2.ap[:], reads=[ko2], writes=[dT["nk"]])
            if sample:
                kb.dma("sp", ktmp.ap[:], ck_d[l, :, hp * 128:(hp + 1) * 128].rearrange("(c p) n -> p c n", p=128), writes=[ktmp])
                pb = pbank()
                for c in range(4):
                    kb.op("pe", lambda e: e.transpose(pb.ap[:, c * 128:(c + 1) * 128], ktmp.ap[:, c, :], ident.ap[:]),
                          reads=[ktmp, ident], writes=[pb])
                kb.op("dve", lambda e: e.tensor_copy(out=kcT.ap[:], in_=pb.ap[:, :]), reads=[pb], writes=[kcT])
                stage_cast(Vc, Vc.ap[:, :, :], cv_d[l, :, hp * 128:(hp + 1) * 128].rearrange("(c p) n -> p c n", p=128))
                for hh in range(2):
                    wsrc = ww_d[l, 2 * hp + hh, :, :].rearrange("p (a b) -> p a b", a=9)
                    wdst = wwt.ap[:, hh, :].rearrange("p (a b) -> p a b", a=9)
                    stage_cast(wwt, wdst[:, 0:5, :], wsrc[:, 0:5, :])
                    stage_cast(wwt, wdst[:, 5:9, :], wsrc[:, 5:9, :])
            for hh in range(2):
                po = hh * 64
                if not sample:
                    for s in range(nseq):
                        b0 = s * L
                        E = rot(Eb)
                        pS = pbank()
                        for kc in range(2):
                            kb.op("pe", lambda e: e.matmul(pS.ap[:, kc * 256:(kc + 1) * 256], lhsT=kT.ap[po:po + 64, b0 + kc * 128:b0 + (kc + 1) * 128],
                                                           rhs=qT.ap[po:po + 64, b0:b0 + 256], start=True, stop=True), reads=[kT, qT], writes=[pS])
                        kb.op("act", lambda e: e.activation(out=E.ap[:, 0:512], in_=pS.ap[:, :], func=AF.Exp), reads=[pS], writes=[E])
                        pO = pbank()
                        pD = pbank()
                        for kc in range(2):
                            kb.op("pe", lambda e: e.matmul(pO.ap[:, 0:256], lhsT=Vt.ap[:, s * 2 + kc, :], rhs=E.ap[:, kc * 256:(kc + 1) * 256],
                                                           start=(kc == 0), stop=(kc == 1)), reads=[Vt, E], writes=[pO])
                        for kc in range(2):
                            kb.op("pe", lambda e: e.matmul(pD.ap[:, 0:256], lhsT=onesb.ap[:, :], rhs=E.ap[:, kc * 256:(kc + 1) * 256],
                                                           start=(kc == 0), stop=(kc == 1)), reads=[onesb, E], writes=[pD])
                        rd = rot(rden)
                        kb.op("dve", lambda e: e.reciprocal(out=rd.ap[po:po + 64, 0:256], in_=pD.ap[po:po + 64, 0:256]), reads=[pD], writes=[rd])
                        kb.op("dve", lambda e: e.tensor_tensor(out=ccst.ap[po:po + 64, b0:b0 + 256], in0=pO.ap[po:po + 64, 0:256],
                                                               in1=rd.ap[po:po + 64, 0:256], op=ALU.mult), reads=[pO, rd], writes=[ccst])
                else:
                    for i in range(16):
                        lo = min(max(i - 2, 0), 11)
                        d0 = lo - i + 4
                        pS1, pS2, pS3 = pbank(), pbank(), pbank()
                        for m in range(5):
                            dst = pS1.ap[:, m * 128:(m + 1) * 128] if m < 4 else pS2.ap[:, 0:128]
                            kb.op("pe", lambda e: e.matmul(dst, lhsT=kT.ap[po:po + 64, (lo + m) * 128:(lo + m + 1) * 128],
                                                           rhs=qT.ap[po:po + 64, i * 128:(i + 1) * 128], start=True, stop=True),
                                  reads=[kT, qT], writes=[pS1 if m < 4 else pS2])
                        for c in range(4):
                            kb.op("pe", lambda e: e.matmul(pS3.ap[:, c * 128:(c + 1) * 128], lhsT=kcT.ap[po:po + 64, c * 128:(c + 1) * 128],
                                                           rhs=qT.ap[po:po + 64, i * 128:(i + 1) * 128], start=True, stop=True),
                                  reads=[kcT, qT], writes=[pS3])
                        S = rot(Sb)
                        E = rot(Eb)
                        kb.op("dve", lambda e: e.tensor_tensor(out=S.ap[:, 0:512], in0=pS1.ap[:, :], in1=wwt.ap[:, hh, d0 * 128:(d0 + 4) * 128],
                                                               op=ALU.add), reads=[pS1, wwt], writes=[S])
                        kb.op("dve", lambda e: e.tensor_tensor(out=S.ap[:, 512:640], in0=pS2.ap[:, 0:128], in1=wwt.ap[:, hh, (d0 + 4) * 128:(d0 + 5) * 128],
                                                               op=ALU.add), reads=[pS2, wwt], writes=[S])
                        kb.op("act", lambda e: e.activation(out=E.ap[:, 0:640], in_=S.ap[:, :], func=AF.Exp), reads=[S], writes=[E])
                        kb.op("act", lambda e: e.activation(out=E.ap[:, 640:1152], in_=pS3.ap[:, :], func=AF.Exp), reads=[pS3], writes=[E])
                        pO, pD = pbank(), pbank()
                        for (pp, use_v) in ((pO, True), (pD, False)):
                            for c in range(4):
                                lh = Vc.ap[:, c, :] if use_v else onesb.ap[:, :]
                                kb.op("pe", lambda e: e.matmul(pp.ap[:, 0:128], lhsT=lh, rhs=E.ap[:, 640 + c * 128:640 + (c + 1) * 128],
                                                               start=(c == 0), stop=False), reads=[Vc, onesb, E], writes=[pp])
                            for b in range(2):
                                r = 2 * i + b
                                s0 = min(max(r - 4, 0), 24)
                                items = []
                                for m in range(5):
                                    aa = [a for a in range(2) if s0 <= 2 * (lo + m) + a < s0 + 8]
                                    if len(aa) == 2:
                                        items.append((m, 0, 128))
                                    elif len(aa) == 1:
                                        items.append((m, aa[0] * 64, aa[0] * 64 + 64))
                                for ii, (m, p0, p1) in enumerate(items):
                                    lh = Vt.ap[p0:p1, lo + m, :] if use_v else onesb.ap[p0:p1, :]
                                    kb.op("pe", lambda e: e.matmul(pp.ap[:, b * 64:(b + 1) * 64], lhsT=lh,
                                                                   rhs=E.ap[p0:p1, m * 128 + b * 64:m * 128 + (b + 1) * 64],
                                                                   start=False, stop=(b == 1 and ii == len(items) - 1)), reads=[Vt, onesb, E], writes=[pp])
                        rd = rot(rden)
                        kb.op("dve", lambda e: e.reciprocal(out=rd.ap[po:po + 64, 0:128], in_=pD.ap[po:po + 64, 0:128]), reads=[pD], writes=[rd])
                        kb.op("dve", lambda e: e.tensor_tensor(out=ccst.ap[po:po + 64, i * 128:(i + 1) * 128], in0=pO.ap[po:po + 64, 0:128],
                                                               in1=rd.ap[po:po + 64, 0:128], op=ALU.mult), reads=[pO, rd], writes=[ccst])
            kb.dma(dq(), ccs_d[:, hp, 0:Tn], ccst.ap[:, :], reads=[ccst], writes=[dT["ccs"]])
        pipelined(8, att_load, att_compute)
        kb.pop()
        ckpt("B1")

        kb.push()
        alloc_pools(4, 5)
        PADC = 15
        Lp = L + 2 * PADC
        glu = kb.sb([128, nseq * Lp], F32)
        cacc = xres
        sig = kb.sb([128, 512], F32)
        tm512 = kb.sb([128, 512], F32)
        tmb = kb.sb([128, 512], BF16)
        cstage = kb.sb([128, ntile, 512], BF16)
        cc4 = [kb.sb([128, 4, 128], BF16) for _ in range(2)]
        gl_v = glu.ap[:, :].rearrange("p (s t) -> p s t", s=nseq)
        ca_v = cacc.ap[:, 0:Tn].rearrange("p (s t) -> p s t", s=nseq)
        def conf_load(c):
            return (load_w(win_d[l, :, 3072 + c * 128:3072 + (c + 1) * 128], 16, 128),
                    load_w(win_d[l, :, 3584 + c * 128:3584 + (c + 1) * 128], 16, 128))

        def conf_compute(c, ws):
            (wu_t, wu), (wg_t, wg) = ws
            kb.op("pool", lambda e: e.memset(glu.ap[:, :], 0.0), writes=[glu])
            for s in range(nseq):
                for t0 in range(0, L, 512):
                    n = min(512, L - t0)
                    pu, pg = pbank(), pbank()
                    wview_t[0] = wu_t
                    proj_fm(wu, 0, s * L + t0, n, pu)
                    wview_t[0] = wg_t
                    proj_fm(wg, 0, s * L + t0, n, pg)
                    kb.op("act", lambda e: e.activation(out=sig.ap[:, 0:n], in_=pg.ap[:, 0:n], func=AF.Sigmoid), reads=[pg], writes=[sig])
                    kb.op("dve", lambda e: e.tensor_tensor(out=gl_v[:, s, PADC + t0:PADC + t0 + n], in0=pu.ap[:, 0:n], in1=sig.ap[:, 0:n],
                                                           op=ALU.mult), reads=[pu, sig], writes=[glu])
            kb.op("dve", lambda e: e.tensor_scalar(out=ca_v, in0=gl_v[:, :, 0:L], scalar1=colp.ap[:, c * 31:c * 31 + 1],
                                                   scalar2=colp.ap[:, 124 + c:125 + c], op0=ALU.mult, op1=ALU.add), reads=[glu, colp], writes=[cacc])
            for k in range(1, 31):
                kb.op("dve", lambda e: e.scalar_tensor_tensor(out=ca_v, in0=gl_v[:, :, k:k + L], scalar=colp.ap[:, c * 31 + k:c * 31 + k + 1],
                                                              in1=ca_v, op0=ALU.mult, op1=ALU.add), reads=[glu, colp, cacc], writes=[cacc])
            for tt in range(ntile):
                pb = pbank()
                kb.op("pe", lambda e: e.transpose(pb.ap[:, 0:128], cacc.ap[:, tt * 128:(tt + 1) * 128], ident.ap[:]), reads=[cacc, ident], writes=[pb])
                kb.op("act", lambda e: e.activation(out=cstage.ap[:, tt, c * 128:(c + 1) * 128], in_=pb.ap[:, 0:128], func=AF.Identity), reads=[pb], writes=[cstage])
        pipelined(4, conf_load, conf_compute)
        for tt in range(ntile):
            kb.op("dve", lambda e: e.tensor_copy(out=tm512.ap[:], in_=cstage.ap[:, tt, :]), reads=[cstage], writes=[tm512])
            layer_norm_rows(tm512, tm512.ap[:, :], 512, rowS, rowS.ap[:, 0:512], rowS, rowS.ap[:, 512:1024], tm512)
            kb.op("act", lambda e: e.activation(out=tmb.ap[:], in_=tm512.ap[:], func=AF.Silu), reads=[tm512], writes=[tmb])
            pbt = kb_psb[tt % 2]
            for c in range(4):
                kb.op("pe", lambda e: e.transpose(pbt.ap[:, c * 128:(c + 1) * 128], tmb.ap[:, c * 128:(c + 1) * 128], identb.ap[:]),
                      reads=[tmb, identb], writes=[pbt])
            c4 = cc4[tt % 2]
            kb.op("dve", lambda e: e.tensor_copy(out=c4.ap[:].rearrange("p a b -> p (a b)"), in_=pbt.ap[:, 0:512]), reads=[pbt], writes=[c4])
            kb.dma(dq(), ccs_d[:, 8:12, tt * 128:(tt + 1) * 128], c4.ap[:], reads=[c4], writes=[dT["ccs"]])
        kb.pop()
        ckpt("B2")

        kb.push()
        alloc_pools(2, 2) if sample else alloc_pools(4, 4)
        PADS = 2
        Ls = L + 2 * PADS
        xcT = kb.sb([128, 8, Tn], BF16)
        xraw = kb.sb([128, nseq * Ls], F32)
        cacc = xres
        ca_v = cacc.ap[:, 0:Tn].rearrange("p (s t) -> p s t", s=nseq)
        dtt = kb.sb([128, ntile, 16], F32)
        dta = kb.sb([128, ntile, 16], F32)
        cs = kb.sb([128, ntile, 16], F32)
        ncs = kb.sb([128, ntile, 16], F32)
        ecs = kb.sb([128, ntile, 16], F32)
        dend = kb.sb([128, ntile, 16], F32)
        cdec = kb.sb([128, ntile, 16], F32)
        tot = kb.sb([128, 16], F32)
        xsTt = [kb.sb([128, 512], BF16) for _ in range(2)]
        Btt = [kb.sb([128, 256], BF16) for _ in range(2)]
        xdtt = [kb.sb([128, 2, 512], BF16) for _ in range(2)]
        xdw = [kb.sb([128, 512], BF16) for _ in range(2)]
        zst = [kb.sb([128, 512], BF16) for _ in range(2)]
        HT = kb.sb([128, 2, 512], F32)
        Hst = [kb.sb([128, 2, 512], BF16) for _ in range(2)]
        Gm = [kb.sb([128, 4, 128], F32) for _ in range(2)]
        A1 = [kb.sb([128, 128], F32) for _ in range(2)]
        Dm = [kb.sb([128, 128], F32) for _ in range(2)]
        Ms = [kb.sb([128, 128], BF16) for _ in range(2)]
        ydsb = kb.sb([128, 512], F32)
        ysb = kb.sb([128, 512], F32)
        sttmp = kb.sb([64, 128], F32)
        ssq = kb.sb([128, 2], F32)
        tmb = kb.sb([128, 512], BF16)
        cc4 = [kb.sb([128, 4, 128], BF16) for _ in range(2)]
        xr_v = xraw.ap[:, :].rearrange("p (s t) -> p s t", s=nseq)
        def xbc_load(c):
            return load_w(win_d[l, :, 4608 + c * 128:4608 + (c + 1) * 128], 16, 128)

        def xbc_compute(c, ws):
            wx_t, wx = ws
            kb.op("pool", lambda e: e.memset(xraw.ap[:, :], 0.0), writes=[xraw])
            for s in range(nseq):
                for t0 in range(0, L, 512):
                    n = min(512, L - t0)
                    pb = pbank()
                    wview_t[0] = wx_t
                    proj_fm(wx, 0, s * L + t0, n, pb)
                    kb.op("act", lambda e: e.activation(out=xr_v[:, s, PADS + t0:PADS + t0 + n], in_=pb.ap[:, 0:n], func=AF.Identity), reads=[pb], writes=[xraw])
            kb.op("dve", lambda e: e.tensor_scalar(out=ca_v, in0=xr_v[:, :, 0:L], scalar1=colp.ap[:, 128 + c * 5:129 + c * 5],
                                                   scalar2=colp.ap[:, 168 + c:169 + c], op0=ALU.mult, op1=ALU.add), reads=[xraw, colp], writes=[cacc])
            for k in range(1, 5):
                kb.op("dve", lambda e: e.scalar_tensor_tensor(out=ca_v, in0=xr_v[:, :, k:k + L], scalar=colp.ap[:, 128 + c * 5 + k:129 + c * 5 + k],
                                                              in1=ca_v, op0=ALU.mult, op1=ALU.add), reads=[xraw, colp, cacc], writes=[cacc])
            kb.op("act", lambda e: e.activation(out=xcT.ap[:, c, :], in_=cacc.ap[:, 0:Tn], func=AF.Silu), reads=[cacc], writes=[xcT])
        pipelined(8, xbc_load, xbc_compute)
        wd_t, wdv = load_w(win_d[l, :, 5632:5648], 16, 16)
        for tt in range(ntile):
            pb = pbank()
            wview_t[0] = wd_t
            proj_tm(wdv, 0, 16, tt, pb)
            kb.op("dve", lambda e: e.tensor_tensor(out=dtt.ap[:, tt, :], in0=pb.ap[:, 0:16], in1=rowS.ap[:, 1552:1568], op=ALU.add),
                  reads=[pb, rowS], writes=[dtt])
        kb.op("act", lambda e: e.activation(out=dtt.ap[:], in_=dtt.ap[:], func=AF.Exp), reads=[dtt], writes=[dtt])
        kb.op("act", lambda e: e.activation(out=dtt.ap[:], in_=dtt.ap[:], func=AF.Ln, bias=1.0), reads=[dtt], writes=[dtt])
        for tt in range(ntile):
            kb.op("dve", lambda e: e.tensor_tensor(out=dta.ap[:, tt, :], in0=dtt.ap[:, tt, :], in1=rowS.ap[:, 1536:1552], op=ALU.mult),
                  reads=[dtt, rowS], writes=[dta])
        for tt in range(ntile):
            pb = pbank()
            kb.op("pe", lambda e: e.matmul(pb.ap[:, 0:8], lhsT=LT.ap[:], rhs=dta.ap[:, tt, 0:8], start=True, stop=True), reads=[LT, dta], writes=[pb])
            kb.op("pe", lambda e: e.matmul(pb.ap[:, 8:16], lhsT=UT.ap[:], rhs=dta.ap[:, tt, 8:16], start=True, stop=True), reads=[UT, dta], writes=[pb])
            kb.op("pe", lambda e: e.matmul(pb.ap[:, 16:32], lhsT=onesf.ap[:], rhs=dta.ap[:, tt, :], start=True, stop=True), reads=[onesf, dta], writes=[pb])
            kb.op("dve", lambda e: e.tensor_copy(out=cs.ap[:, tt, :], in_=pb.ap[:, 0:16]), reads=[pb], writes=[cs])
            kb.op("dve", lambda e: e.tensor_copy(out=tot.ap[:], in_=pb.ap[:, 16:32]), reads=[pb], writes=[tot])
            kb.op("dve", lambda e: e.tensor_scalar_mul(out=ncs.ap[:, tt, :], in0=cs.ap[:, tt, :], scalar1=-1.0), reads=[cs], writes=[ncs])
            kb.op("act", lambda e: e.activation(out=ecs.ap[:, tt, :], in_=cs.ap[:, tt, :], func=AF.Exp), reads=[cs], writes=[ecs])
            kb.op("act", lambda e: e.activation(out=cdec.ap[:, tt, :], in_=tot.ap[:], func=AF.Exp), reads=[tot], writes=[cdec])
            kb.op("dve", lambda e: e.tensor_tensor(out=dend.ap[:, tt, :], in0=tot.ap[:], in1=cs.ap[:, tt, :], op=ALU.subtract),
                  reads=[tot, cs], writes=[dend])
        kb.op("act", lambda e: e.activation(out=dend.ap[:], in_=dend.ap[:], func=AF.Exp), reads=[dend], writes=[dend])

        def tok_major(tt, need_b):
            xs_ = rot(xsTt)
            pbt = kb_psb[0]
            for c in range(4):
                kb.op("pe", lambda e: e.transpose(pbt.ap[:, c * 128:(c + 1) * 128], xcT.ap[:, c, tt * 128:(tt + 1) * 128], identb.ap[:]),
                      reads=[xcT, identb], writes=[pbt])
            kb.op("act", lambda e: e.activation(out=xs_.ap[:], in_=pbt.ap[:, 0:512], func=AF.Identity), reads=[pbt], writes=[xs_])
            b_ = None
            if need_b:
                b_ = rot(Btt)
                pbt2 = kb_psb[1]
                for c in range(2):
                    kb.op("pe", lambda e: e.transpose(pbt2.ap[:, c * 128:(c + 1) * 128], xcT.ap[:, 4 + c, tt * 128:(tt + 1) * 128], identb.ap[:]),
                          reads=[xcT, identb], writes=[pbt2])
                kb.op("act", lambda e: e.activation(out=b_.ap[:], in_=pbt2.ap[:, 0:256], func=AF.Identity), reads=[pbt2], writes=[b_])
            return xs_, b_

        def mk_xdt(xs_, tt, dr, dst_ap, dst_t):
            kb.op("dve", lambda e: e.tensor_tensor(out=dst_ap.rearrange("p (h q) -> p h q", h=8),
                                                   in0=xs_.ap[:].rearrange("p (h q) -> p h q", h=8),
                                                   in1=dtt.ap[:, tt, dr * 8:(dr + 1) * 8].unsqueeze(2).to_broadcast([128, 8, 64]), op=ALU.mult),
                  reads=[xs_, dtt], writes=[dst_t])

        nch = L // 128
        for s in range(nseq):
            if sample:
                for dr in range(2):
                    for h in range(8):
                        kb.dma(dq(), sttmp.ap[:], st_d[l, dr, h, :, :], writes=[sttmp])
                        pb = pbank()
                        kb.op("pe", lambda e: e.transpose(pb.ap[:, 0:64], sttmp.ap[:, :], ident.ap[0:64, 0:64]), reads=[sttmp, ident], writes=[pb])
                        kb.op("dve", lambda e: e.tensor_copy(out=HT.ap[:, dr, h * 64:(h + 1) * 64], in_=pb.ap[:, 0:64]), reads=[pb], writes=[HT])
            else:
                kb.op("pool", lambda e: e.memset(HT.ap[:], 0.0), writes=[HT])
            for dr in range(2):
                order = range(nch) if dr == 0 else range(nch - 1, -1, -1)
                for cch in order:
                    tt = s * nch + cch
                    hst = rot(Hst)
                    kb.op("act", lambda e: e.activation(out=hst.ap[:, dr, :], in_=HT.ap[:, dr, :], func=AF.Identity), reads=[HT], writes=[hst])
                    kb.dma(dq(), hs_d[cch, :, dr, :], hst.ap[:, dr, :], reads=[hst], writes=[dT["hs"]])
                    xs_, b_ = tok_major(tt, True)
                    xd = rot(xdtt)
                    mk_xdt(xs_, tt, dr, xd.ap[:, 0, :], xd)
                    xw = rot(xdw)
                    kb.op("dve", lambda e: e.tensor_tensor(out=xw.ap[:, :].rearrange("p (h q) -> p h q", h=8),
                                                           in0=xd.ap[:, 0, :].rearrange("p (h q) -> p h q", h=8),
                                                           in1=dend.ap[:, tt, dr * 8:(dr + 1) * 8].unsqueeze(2).to_broadcast([128, 8, 64]), op=ALU.mult),
                          reads=[xd, dend], writes=[xw])
                    pb = pbank()
                    for g in range(2):
                        kb.op("pe", lambda e: e.matmul(pb.ap[:, g * 256:(g + 1) * 256], lhsT=b_.ap[:, g * 128:(g + 1) * 128],
                                                       rhs=xw.ap[:, g * 256:(g + 1) * 256], start=True, stop=True), reads=[b_, xw], writes=[pb])
                    kb.op("dve", lambda e: e.tensor_tensor(out=HT.ap[:, dr, :].rearrange("p (h q) -> p h q", h=8),
                                                           in0=HT.ap[:, dr, :].rearrange("p (h q) -> p h q", h=8),
                                                           in1=cdec.ap[:, tt, dr * 8:(dr + 1) * 8].unsqueeze(2).to_broadcast([128, 8, 64]), op=ALU.mult),
                          reads=[HT, cdec], writes=[HT])
                    kb.op("dve", lambda e: e.tensor_tensor(out=HT.ap[:, dr, :], in0=HT.ap[:, dr, :], in1=pb.ap[:, :], op=ALU.add),
                          reads=[HT, pb], writes=[HT])
                if not sample:
                    for h in range(8):
                        pb = pbank()
                        kb.op("pe", lambda e: e.transpose(pb.ap[0:64, 0:128], HT.ap[:, dr, h * 64:(h + 1) * 64], ident.ap[:]), reads=[HT, ident], writes=[pb])
                        kb.op("dve", lambda e: e.tensor_copy(out=sttmp.ap[:], in_=pb.ap[0:64, 0:128]), reads=[pb], writes=[sttmp])
                        kb.dma(dq(), ns_d[s, l, dr, h, :, :], sttmp.ap[:], reads=[sttmp], writes=[dT["ns"]])
            wz_t, wz = load_w(win_d[l, :, 4096:4352], 16, 256)
            wz2_t, wz2 = load_w(win_d[l, :, 4352:4608], 16, 256)
            for cch in range(nch):
                tt = s * nch + cch
                tok0 = tt * 128
                xs_, _ = tok_major(tt, False)
                xd = rot(xdtt)
                mk_xdt(xs_, tt, 0, xd.ap[:, 0, :], xd)
                mk_xdt(xs_, tt, 1, xd.ap[:, 1, :], xd)
                zt = rot(zst)
                for (wzt_, wzv_, c0) in ((wz_t, wz, 0), (wz2_t, wz2, 256)):
                    pb = pbank()
                    wview_t[0] = wzt_
                    proj_tm(wzv_, 0, 256, tt, pb)
                    kb.op("act", lambda e: e.activation(out=zt.ap[:, c0:c0 + 256], in_=pb.ap[:, 0:256], func=AF.Silu), reads=[pb], writes=[zt])
                hst = rot(Hst)
                kb.dma(dq(), hst.ap[:], hs_d[cch, :, :, :], reads=[dT["hs"]], writes=[hst])
                gm = rot(Gm)
                for g in range(2):
                    pb = pbank()
                    kb.op("pe", lambda e: e.matmul(pb.ap[:, 0:128], lhsT=xcT.ap[:, 4 + g, tok0:tok0 + 128], rhs=xcT.ap[:, 6 + g, tok0:tok0 + 128],
                                                   start=True, stop=True), reads=[xcT], writes=[pb])
                    kb.op("dve", lambda e: e.tensor_tensor(out=gm.ap[:, g * 2, :], in0=pb.ap[:, 0:128], in1=LT.ap[:], op=ALU.mult), reads=[pb, LT], writes=[gm])
                    kb.op("dve", lambda e: e.tensor_tensor(out=gm.ap[:, g * 2 + 1, :], in0=pb.ap[:, 0:128], in1=UT.ap[:], op=ALU.mult), reads=[pb, UT], writes=[gm])
                pYd, pYf, pYb = pbank(), pbank(), pbank()
                pinned.update((id(pYd), id(pYf), id(pYb)))
                for h in range(8):
                    g = h // 4
                    for dr in range(2):
                        col = dr * 8 + h
                        a1, dm, ms = rot(A1), rot(Dm), rot(Ms)
                        kb.op("pool", lambda e: e.tensor_scalar_mul(out=a1.ap[:], in0=onesf.ap[:], scalar1=dta.ap[:, tt, col:col + 1]),
                              reads=[onesf, dta], writes=[a1])
                        pb = pbank()
                        kb.op("pe", lambda e: e.matmul(pb.ap[:, 0:128], lhsT=a1.ap[:], rhs=(LT if dr == 0 else UT).ap[:], start=True, stop=True),
                              reads=[a1, LT, UT], writes=[pb])
                        kb.op("dve", lambda e: e.tensor_scalar(out=dm.ap[:], in0=pb.ap[:, 0:128], scalar1=ncs.ap[:, tt, col:col + 1], scalar2=0.0,
                                                               op0=ALU.add, op1=ALU.min), reads=[pb, ncs], writes=[dm])
                        kb.op("act", lambda e: e.activation(out=dm.ap[:], in_=dm.ap[:], func=AF.Exp), reads=[dm], writes=[dm])
                        kb.op("dve", lambda e: e.tensor_tensor(out=ms.ap[:], in0=dm.ap[:], in1=gm.ap[:, g * 2 + dr, :], op=ALU.mult),
                              reads=[dm, gm], writes=[ms])
                        kb.op("pe", lambda e: e.matmul(pYd.ap[:, h * 64:(h + 1) * 64], lhsT=ms.ap[:], rhs=xd.ap[:, dr, h * 64:(h + 1) * 64],
                                                       start=(dr == 0), stop=(dr == 1)), reads=[ms, xd], writes=[pYd])
                        pY = pYf if dr == 0 else pYb
                        kb.op("pe", lambda e: e.matmul(pY.ap[:, h * 64:(h + 1) * 64], lhsT=xcT.ap[:, 6 + g, tok0:tok0 + 128],
                                                       rhs=hst.ap[:, dr, h * 64:(h + 1) * 64], start=True, stop=True), reads=[xcT, hst], writes=[pY])
                pinned.clear()
                kb.op("act", lambda e: e.activation(out=ydsb.ap[:], in_=pYd.ap[:, :], func=AF.Identity), reads=[pYd], writes=[ydsb])
                for dr, pY in ((0, pYf), (1, pYb)):
                    kb.op("dve", lambda e: e.tensor_tensor(out=ysb.ap[:].rearrange("p (h q) -> p h q", h=8),
                                                           in0=pY.ap[:, :].rearrange("p (h q) -> p h q", h=8),
                                                           in1=ecs.ap[:, tt, dr * 8:(dr + 1) * 8].unsqueeze(2).to_broadcast([128, 8, 64]), op=ALU.mult),
                          reads=[pY, ecs], writes=[ysb])
                    kb.op("dve", lambda e: e.tensor_tensor(out=ydsb.ap[:], in0=ydsb.ap[:], in1=ysb.ap[:], op=ALU.add), reads=[ydsb, ysb], writes=[ydsb])
                kb.op("dve", lambda e: e.tensor_tensor(out=ysb.ap[:], in0=xs_.ap[:], in1=dexp.ap[:], op=ALU.mult), reads=[xs_, dexp], writes=[ysb])
                kb.op("dve", lambda e: e.tensor_tensor(out=ydsb.ap[:], in0=ydsb.ap[:], in1=ysb.ap[:], op=ALU.add), reads=[ydsb, ysb], writes=[ydsb])
                kb.op("dve", lambda e: e.tensor_tensor(out=ydsb.ap[:], in0=ydsb.ap[:], in1=zt.ap[:], op=ALU.mult), reads=[ydsb, zt], writes=[ydsb])
                kb.op("act", lambda e: e.activation(out=ysb.ap[:], in_=ydsb.ap[:], func=AF.Square, accum_out=ssq.ap[:, 0:1]), reads=[ydsb], writes=[ysb, ssq])
                kb.op("dve", lambda e: e.tensor_scalar(out=ssq.ap[:, 1:2], in0=ssq.ap[:, 0:1], scalar1=1.0 / 512, scalar2=EPS, op0=ALU.mult, op1=ALU.add),
                      reads=[ssq], writes=[ssq])
                kb.op("act", lambda e: e.activation(out=ssq.ap[:, 1:2], in_=ssq.ap[:, 1:2], func=AF.Sqrt), reads=[ssq], writes=[ssq])
                kb.op("dve", lambda e: e.reciprocal(out=ssq.ap[:, 1:2], in_=ssq.ap[:, 1:2]), reads=[ssq], writes=[ssq])
                kb.op("dve", lambda e: e.scalar_tensor_tensor(out=tmb.ap[:], in0=ydsb.ap[:], scalar=ssq.ap[:, 1:2], in1=rowS.ap[:, 1024:1536],
                                                              op0=ALU.mult, op1=ALU.mult), reads=[ydsb, ssq, rowS], writes=[tmb])
                pbt = kb_psb[1]
                for c in range(4):
                    kb.op("pe", lambda e: e.transpose(pbt.ap[:, c * 128:(c + 1) * 128], tmb.ap[:, c * 128:(c + 1) * 128], identb.ap[:]),
                          reads=[tmb, identb], writes=[pbt])
                c4 = cc4[tt % 2]
                kb.op("act", lambda e: e.activation(out=c4.ap[:].rearrange("p a b -> p (a b)"), in_=pbt.ap[:, 0:512], func=AF.Identity),
                      reads=[pbt], writes=[c4])
                kb.dma(dq(), ccs_d[:, 12:16, tok0:tok0 + 128], c4.ap[:], reads=[c4], writes=[dT["ccs"]])
        kb.pop()
        ckpt("B3")

        kb.push()
        bigB = kb.sb([128, 16384], BF16)
        growA = kb.sb([128, D], F32)
        lngA = kb.sb([128, D], F32)
        lnbA = kb.sb([128, D], F32)
        raws = [kb.sb([128, 520], F32) for _ in range(4)]
        kb.dma("sp", growA.ap[:], modd[l, crow:crow + 1, 2 * D:3 * D].partition_broadcast(128), reads=[dT["modd"]], writes=[growA])
        kb.dma("sp", lngA.ap[:], rowp_d[l:l + 1, 0:D].partition_broadcast(128), writes=[lngA])
        kb.dma("sp", lnbA.ap[:], rowp_d[l:l + 1, D:2 * D].partition_broadcast(128), writes=[lnbA])
        pre_v = bigA.ap[:, 0:16384].bitcast(F32).rearrange("p (a n) -> p a n", a=4)
        ccg_v = bigB.ap[:, 0:8192].rearrange("p (k t) -> p k t", k=16)
        for g0 in range(0, Tn, 512):
            kb.push()
            alloc_pools(4, 3)
            kb.dma(dq(), ccg_v, ccs_d[:, :, g0:g0 + 512], reads=[dT["ccs"]], writes=[bigB])
            def c_load(cg):
                wts = []
                for hf in range(2):
                    wb = wbuf()
                    wv_ = wb.ap[:, 0:2048].rearrange("p (c n) -> p c n", c=8)
                    stage_cast(wb, wv_, wout_d[l, hf * 1024:(hf + 1) * 1024, cg * 256:(cg + 1) * 256].rearrange("(c p) n -> p c n", p=128))
                    wts.append((wb, wv_))
                return wts

            def c_compute(cg, wts):
                for ti in range(4):
                    pb = pbank()
                    for k in range(16):
                        wb, wv_ = wts[k // 8]
                        kb.op("pe", lambda e: e.matmul(pb.ap[:, 0:256], lhsT=ccg_v[:, k, ti * 128:(ti + 1) * 128], rhs=wv_[:, k % 8, :],
                                                       start=(k == 0), stop=(k == 15)), reads=[bigB, wb], writes=[pb])
                    kb.op("dve", lambda e: e.tensor_tensor(out=pre_v[:, ti, cg * 256:(cg + 1) * 256], in0=pb.ap[:, 0:256], in1=growA.ap[:, cg * 256:(cg + 1) * 256],
                                                           op=ALU.mult), reads=[pb, growA], writes=[bigA])
            pipelined(8, c_load, c_compute)
            kb.pop()
            kb.push()
            xr2 = [xres, kb.sb([128, D], F32)]
            for ti in range(4):
                r0 = g0 + ti * 128
                xr = xr2[ti % 2]
                kb.dma(dq(), xr.ap[:], xin_d[r0:r0 + 128, :], reads=[xin_T], writes=[xr])
                kb.op("dve", lambda e: e.scalar_tensor_tensor(out=pre_v[:, ti, :], in0=xr.ap[:], scalar=ALPHA, in1=pre_v[:, ti, :], op0=ALU.mult, op1=ALU.add),
                      reads=[xr, bigA], writes=[bigA])
                layer_norm_rows(bigA, pre_v[:, ti, :], D, lngA, lngA.ap[:], lnbA, lnbA.ap[:], xr)
                kb.dma(dq(), x1_d[r0:r0 + 128, :], xr.ap[:], reads=[xr], writes=[x1_T])
            kb.pop()
        ckpt("C")

        kb.dma("sp", growA.ap[:], modd[l, crow:crow + 1, 5 * D:6 * D].partition_broadcast(128), reads=[dT["modd"]], writes=[growA])
        kb.dma("sp", lngA.ap[:], rowp_d[l:l + 1, 2 * D:3 * D].partition_broadcast(128), writes=[lngA])
        kb.dma("sp", lnbA.ap[:], rowp_d[l:l + 1, 3 * D:4 * D].partition_broadcast(128), writes=[lnbA])
        TG = 512
        W2 = TG + 2
        nseg = 1 if sample else 2
        SL = TG // nseg
        RW = nseg * (SL + 2)
        h2_v = bigA.ap[:, 0:16 * W2].rearrange("p (k t) -> p k t", k=16)
        act_v = bigA.ap[:, 16 * W2:16 * W2 + 44 * TG].rearrange("p (k t) -> p k t", k=44)
        h2T = T(bigA.ap[:, 0:16 * W2])
        actT = T(bigA.ap[:, 16 * W2:16 * W2 + 44 * TG])
        ff_v = bigB.ap[:, :].bitcast(F32).rearrange("p (a n) -> p a n", a=4)
        ra_t, rg_t, aa_t, ag_t = raws
        for g0 in range(0, Tn, TG):
            lh = sample and g0 > 0
            rh = sample and (g0 + TG < Tn)
            kb.op("pool", lambda e: e.memset(bigA.ap[:, 0:16 * W2], 0.0), writes=[h2T])
            for ti in range(TG // 128):
                build_hT(h2T, h2_v, x1_d[g0 + ti * 128:g0 + (ti + 1) * 128, :], x1_T, 128, 2 + ti * 128, 2, 3)
            if lh:
                build_hT(h2T, h2_v, x1_d[g0 - 1:g0, :], x1_T, 1, 0, 2, 3)
            if rh:
                build_hT(h2T, h2_v, x1_d[g0 + TG:g0 + TG + 1, :], x1_T, 1, 1, 2, 3)
            kb.op("pool", lambda e: e.memset(ra_t.ap[:, 0:RW], 0.0), writes=[ra_t])
            kb.op("pool", lambda e: e.memset(rg_t.ap[:, 0:RW], 0.0), writes=[rg_t])
            kb.push()
            alloc_pools(3, 4)
            def up_load(j):
                wb = wbuf()
                wa = wb.ap[:, 0:2048].rearrange("p (c n) -> p c n", c=16)
                wg = wb.ap[:, 2048:4096].rearrange("p (c n) -> p c n", c=16)
                stage_cast(wb, wa, wup_d[l, :, j * 128:(j + 1) * 128].rearrange("(c p) n -> p c n", p=128))
                stage_cast(wb, wg, wup_d[l, :, DFF + j * 128:DFF + (j + 1) * 128].rearrange("(c p) n -> p c n", p=128))
                return wb, wa, wg

            def up_compute(j, ws):
                wb, wa, wg = ws
                for (wv_, rt) in ((wa, ra_t), (wg, rg_t)):
                    pb = pbank()
                    for k in range(16):
                        kb.op("pe", lambda e: e.matmul(pb.ap[:, 0:TG], lhsT=wv_[:, k, :], rhs=h2_v[:, k, 2:TG + 2], start=(k == 0), stop=(k == 15)),
                              reads=[wb, h2T], writes=[pb])
                    kb.op("act", lambda e: e.activation(out=rt.ap[:, 0:RW].rearrange("p (s t) -> p s t", s=nseg)[:, :, 1:SL + 1],
                                                        in_=pb.ap[:, 0:TG].rearrange("p (s t) -> p s t", s=nseg), func=AF.Identity), reads=[pb], writes=[rt])
                    if lh or rh:
                        pb = pbank()
                        for k in range(16):
                            kb.op("pe", lambda e: e.matmul(pb.ap[:, 0:2], lhsT=wv_[:, k, :], rhs=h2_v[:, k, 0:2],
                                                           start=(k == 0), stop=(k == 15)), reads=[wb, h2T], writes=[pb])
                        if lh:
                            kb.op("act", lambda e: e.activation(out=rt.ap[:, 0:1], in_=pb.ap[:, 0:1], func=AF.Identity), reads=[pb], writes=[rt])
                        if rh:
                            kb.op("act", lambda e: e.activation(out=rt.ap[:, TG + 1:TG + 2], in_=pb.ap[:, 1:2], func=AF.Identity), reads=[pb], writes=[rt])
                for (rt, at, cj) in ((ra_t, aa_t, j), (rg_t, ag_t, 44 + j)):
                    rv3 = rt.ap[:, 0:RW].rearrange("p (s t) -> p s t", s=nseg)
                    av3 = at.ap[:, 0:TG].rearrange("p (s t) -> p s t", s=nseg)
                    kb.op("dve", lambda e: e.tensor_scalar(out=av3, in0=rv3[:, :, 0:SL], scalar1=colp.ap[:, 176 + cj * 3:177 + cj * 3], scalar2=None,
                                                           op0=ALU.mult), reads=[rt, colp], writes=[at])
                    for k in (1, 2):
                        kb.op("dve", lambda e: e.scalar_tensor_tensor(out=av3, in0=rv3[:, :, k:k + SL], scalar=colp.ap[:, 176 + cj * 3 + k:177 + cj * 3 + k],
                                                                      in1=av3, op0=ALU.mult, op1=ALU.add), reads=[rt, colp, at], writes=[at])
                kb.op("act", lambda e: e.activation(out=ag_t.ap[:, 0:TG], in_=ag_t.ap[:, 0:TG], func=AF.Silu, bias=colp.ap[:, 440 + 44 + j:441 + 44 + j]),
                      reads=[ag_t, colp], writes=[ag_t])
                kb.op("dve", lambda e: e.scalar_tensor_tensor(out=act_v[:, j, :], in0=aa_t.ap[:, 0:TG], scalar=colp.ap[:, 440 + j:441 + j],
                                                              in1=ag_t.ap[:, 0:TG], op0=ALU.add, op1=ALU.mult), reads=[aa_t, ag_t, colp], writes=[actT])
            pipelined3(44, up_load, up_compute)
            kb.pop()
            kb.push()
            alloc_pools(4, 3)
            def dn_load(cg):
                wts = []
                for qq in range(2):
                    wb = wbuf()
                    wv_ = wb.ap[:, 0:22 * 128].rearrange("p (c n) -> p c n", c=22)
                    srcv_ = wdn_d[l, qq * 2816:(qq + 1) * 2816, cg * 128:(cg + 1) * 128].rearrange("(c p) n -> p c n", p=128)
                    stage_cast(wb, wv_[:, 0:11, :], srcv_[:, 0:11, :])
                    stage_cast(wb, wv_[:, 11:22, :], srcv_[:, 11:22, :])
                    wts.append((wb, wv_))
                return wts

            def dn_compute(cg, wts):
                for ti in range(TG // 128):
                    pb = pbank()
                    for k in range(44):
                        wb, wv_ = wts[k // 22]
                        kb.op("pe", lambda e: e.matmul(pb.ap[:, 0:128], lhsT=act_v[:, k, ti * 128:(ti + 1) * 128], rhs=wv_[:, k % 22, :],
                                                       start=(k == 0), stop=(k == 43)), reads=[actT, wb], writes=[pb])
                    kb.op("dve", lambda e: e.tensor_tensor(out=ff_v[:, ti, cg * 128:(cg + 1) * 128], in0=pb.ap[:, 0:128], in1=growA.ap[:, cg * 128:(cg + 1) * 128],
                                                           op=ALU.mult), reads=[pb, growA], writes=[bigB])
            pipelined(16, dn_load, dn_compute)
            kb.pop()
            kb.push()
            xr2 = [xres, kb.sb([128, D], F32)]
            for ti in range(TG // 128):
                r0 = g0 + ti * 128
                xr = xr2[ti % 2]
                kb.dma(dq(), xr.ap[:], x1_d[r0:r0 + 128, :], reads=[x1_T], writes=[xr])
                kb.op("dve", lambda e: e.scalar_tensor_tensor(out=ff_v[:, ti, :], in0=xr.ap[:], scalar=ALPHA, in1=ff_v[:, ti, :], op0=ALU.mult, op1=ALU.add),
                      reads=[xr, bigB], writes=[bigB])
                layer_norm_rows(bigB, ff_v[:, ti, :], D, lngA, lngA.ap[:], lnbA, lnbA.ap[:], xr)
                kb.dma(dq(), xout_d[r0:r0 + 128, :], xr.ap[:], reads=[xr], writes=[xout_T])
            kb.pop()
        kb.pop()

    try:
        ckpt("mod")
        if "P" in JOBS:
            run_job(0, xp_d, T(xp_d), x1p_d, dT["x1p"], x2p_d, dT["x2p"], 512, 2, 256, 0, False)
        if "S" in JOBS:
            run_job(0, xs_d, T(xs_d), x1s_d, dT["x1s"], x2s_d, dT["x2s"], 2048, 1, 2048, 1, True)
        ckpt("L0")
        if "P" in JOBS:
            run_job(1, x2p_d, dT["x2p"], x1p_d, dT["x1p"], yp_d, dT["yp"], 512, 2, 256, 0, False)
        if "S" in JOBS:
            run_job(1, x2s_d, dT["x2s"], x1s_d, dT["x1s"], ys_d, dT["ys"], 2048, 1, 2048, 1, True)
    except StopBuild:
        if STOP in ("L0", "C") and "S" in JOBS:
            dump("x2s", dT["x2s"], x2s_d[:, :])
            dump("x1s", dT["x1s"], x1s_d[:, :])
        if STOP in ("B1", "B2", "B3"):
            nck = {"B1": 8, "B2": 12, "B3": 16}[STOP]
            dump("ccs", dT["ccs"], ccs_d[:, 0:nck, :], BF16)
    kb.finish()
    return kb.nc


JOBS = "PS"


def _na_bias_table(rpb):
    L = rpb.shape[0]
    a = np.arange(2)[:, None, None, None, None]
    kc = np.arange(64)[None, :, None, None, None]
    d = np.arange(-4, 5)[None, None, :, None, None]
    b = np.arange(2)[None, None, None, :, None]
    qc = np.arange(64)[None, None, None, None, :]
    dr = 2 * d + a - b
    cstart = np.clip(qc - 8, 0, 48)
    ok = (kc >= cstart) & (kc < cstart + 16) & (np.abs(dr) <= 7)
    ri = np.clip(dr + 7, 0, 14) + 0 * kc + 0 * qc
    ci = np.clip(kc - qc + 15, 0, 30) + 0 * dr
    ok = np.broadcast_to(ok, ri.shape)
    out = np.full((L, 16) + ri.shape, NEG, np.float32)
    g = rpb[:, :, ri, ci]
    out[:, :, ok] = g[:, :, ok]
    return np.ascontiguousarray(out.reshape(L, 16, 128, 9 * 128))


def _prep(inputs):
    f = lambda k: np.ascontiguousarray(np.asarray(inputs[k], dtype=np.float32))
    L = 2
    colp = np.zeros((L, 128, 540), np.float32)
    cw = f("conv_w")
    colp[:, :, 0:124] = cw.reshape(L, 31, 4, 128).transpose(0, 3, 2, 1).reshape(L, 128, 124)
    colp[:, :, 124:128] = f("conv_b").reshape(L, 4, 128).transpose(0, 2, 1)
    colp[:, :, 128:168] = f("ssm_conv_w").reshape(L, 5, 8, 128).transpose(0, 3, 2, 1).reshape(L, 128, 40)
    colp[:, :, 168:176] = f("ssm_conv_b").reshape(L, 8, 128).transpose(0, 2, 1)
    colp[:, :, 176:440] = f("ffn_conv_w").reshape(L, 3, 88, 128).transpose(0, 3, 2, 1).reshape(L, 128, 264)
    colp[:, :, 440:528] = f("ffn_conv_b").reshape(L, 88, 128).transpose(0, 2, 1)
    rowp = np.zeros((L, 10272), np.float32)
    rowp[:, 0:2048] = f("ln1_g"); rowp[:, 2048:4096] = f("ln1_b")
    rowp[:, 4096:6144] = f("ln2_g"); rowp[:, 6144:8192] = f("ln2_b")
    rowp[:, 8192:8704] = f("conv_ln_g"); rowp[:, 8704:9216] = f("conv_ln_b")
    rowp[:, 9216:9728] = f("ssm_norm_g")
    rowp[:, 9728:9744] = f("ssm_a_log").reshape(L, 16)
    rowp[:, 9744:9760] = f("ssm_dt_bias").reshape(L, 16)
    rowp[:, 9760:10272] = np.repeat(f("ssm_d"), 64, axis=1)
    return colp, rowp, _na_bias_table(f("rpb"))


def _run(inputs, ncores=8):
    f = lambda k: np.ascontiguousarray(np.asarray(inputs[k], dtype=np.float32))
    colp, rowp, ww = _prep(inputs)
    shared = {"w_mod": f("w_mod"), "b_mod": f("b_mod"), "w_in": f("w_in"), "w_out": f("w_out"), "w_up": f("w_up"),
              "w_down": f("w_down"), "ww": ww, "colp": colp, "rowp": rowp}
    xs, xp, ck, cv, st, c, cctx = f("x_sample"), f("x_prompt"), f("cache_k"), f("cache_v"), f("state_ssm"), f("c"), f("c_ctx")
    in_maps = []
    for core in range(ncores):
        b = core // 2
        m = dict(shared)
        m["xs"] = xs[b]
        m["xp"] = np.ascontiguousarray(xp[2 * core:2 * core + 2].reshape(512, 2048))
        m["ck"] = np.ascontiguousarray(ck[b].reshape(2, 512, 1024))
        m["cv"] = np.ascontiguousarray(cv[b].reshape(2, 512, 1024))
        m["st"] = st[b]
        m["cvec"] = np.ascontiguousarray(np.stack([cctx, c[b]], axis=0))
        in_maps.append(m)
    nc = build()
    res = run_bass_kernel_spmd(nc, in_maps, core_ids=list(range(ncores)))
    return res.results


def kernel(**inputs):
    R = _run(inputs, 8)
    y_prompt = np.concatenate([R[i]["yp"].reshape(2, 256, 2048) for i in range(8)], axis=0)
    y_sample = np.stack([R[2 * b]["ys"] for b in range(4)], axis=0)
    nk = np.concatenate([R[i]["nk"].reshape(2, 2, 256, 16, 64) for i in range(8)], axis=0)
    nv = np.concatenate([R[i]["nv"].reshape(2, 2, 256, 16, 64) for i in range(8)], axis=0)
    ns = np.concatenate([R[i]["ns"] for i in range(8)], axis=0)
    return (y_prompt.astype(np.float32), y_sample.astype(np.float32), nk.astype(np.float32), nv.astype(np.float32), ns.astype(np.float32))
```

```python
import numpy as np
from contextlib import ExitStack
import concourse.bass as bass
import concourse.mybir as mybir
from concourse.bass_utils import run_bass_kernel_spmd

F32 = mybir.dt.float32
BF16 = mybir.dt.bfloat16
AF = mybir.ActivationFunctionType
ALU = mybir.AluOpType

N_DMA_SEMS = 24
D = 2048
DIN = 5648
DFF = 5632
ALPHA = 4 ** 0.25
EPS = 1e-5
NEG = -30000.0


class T:
    __slots__ = ("ap", "w", "r", "psem", "psum")

    def __init__(self, ap):
        self.ap = ap
        self.w = None
        self.r = []
        self.psum = False


class KB:
    def __init__(self):
        self.nc = bass.Bass("TRN2", target_bir_lowering=False)
        nc = self.nc
        self.es = ExitStack()
        self.eng = {"pe": nc.tensor, "act": nc.scalar, "dve": nc.vector, "pool": nc.gpsimd, "sp": nc.sync}
        self.sem, self.cnt = {}, {}
        for e in self.eng:
            self.sem[e] = self.es.enter_context(nc.semaphore("s_" + e))
            self.cnt[e] = 0
        for i in range(N_DMA_SEMS):
            k = "d%d" % i
            self.sem[k] = self.es.enter_context(nc.semaphore("s_" + k))
            self.cnt[k] = 0
        for i in range(3):
            k = "q%d" % i
            self.sem[k] = self.es.enter_context(nc.semaphore("s_" + k))
            self.cnt[k] = 0
        self.dma_rr = 0
        self.seen = {e: {} for e in self.eng}
        self.uid = 0
        self.dq = 0
        self.stack = [self.es]

    def sb(self, shape, dt=F32):
        self.uid += 1
        return T(self.stack[-1].enter_context(self.nc.sbuf_tensor("sb%d" % self.uid, list(shape), dt)))

    def ps(self, shape, dt=F32):
        self.uid += 1
        t = T(self.es.enter_context(self.nc.psum_tensor("ps%d" % self.uid, list(shape), dt)))
        t.psum = True
        return t

    def barrier(self):
        toks = [(k, v) for k, v in self.cnt.items() if v > 0]
        for e in self.eng:
            for tok in toks:
                self._wait(e, tok)

    def push(self):
        es = ExitStack()
        self.stack.append(es)
        return es

    def pop(self):
        self.barrier()
        self.stack.pop().close()

    def dram(self, name, shape, dt=F32, kind="Internal"):
        return self.nc.dram_tensor(name, list(shape), dt, kind=kind).ap()

    def _wait(self, e, tok):
        if tok is None:
            return
        if isinstance(tok, list):
            for t_ in tok:
                self._wait(e, t_)
            return
        k, v = tok
        if k == e and e == "pe":
            return
        if not k.startswith("q") and self.seen[e].get(k, 0) >= v:
            return
        self.eng[e].wait_ge(self.sem[k], v)
        self.seen[e][k] = v

    def _deps(self, e, reads, writes):
        need = {}

        def add(tok):
            if tok is None:
                return
            if isinstance(tok, list):
                for t_ in tok:
                    add(t_)
                return
            k, v = tok
            if need.get(k, 0) < v:
                need[k] = v
        for t in reads:
            add(t.w)
            if t.psum:
                for tok in t.r:
                    if tok[0] != e:
                        add(tok)
        for t in writes:
            add(t.w)
            for tok in t.r:
                add(tok)
        for k, v in need.items():
            self._wait(e, (k, v))

    def pdma(self, wb, out, in_):
        e = "pool"
        self._deps(e, [], [wb])
        ins_c = self.eng[e].sem_clear(self.sem[wb.psem])
        self.cnt[e] += 1
        ins_c.then_inc(self.sem[e], 1)
        ctok = (e, self.cnt[e])
        ins = self.eng[e].dma_start(out=out, in_=in_)
        ins.then_inc(self.sem[wb.psem], 16)
        wb.w = [ctok, (wb.psem, 16)]
        wb.r = []
        return ins

    def _mark(self, tok, reads, writes):
        for t in reads:
            t.r.append(tok)
            if len(t.r) > 16:
                d = {}
                for k, v in t.r:
                    d[k] = max(d.get(k, 0), v)
                t.r = list(d.items())
        for t in writes:
            t.w = tok
            t.r = []

    def op(self, e, fn, reads=(), writes=()):
        self._deps(e, reads, writes)
        ins = fn(self.eng[e])
        self.cnt[e] += 1
        ins.then_inc(self.sem[e], 1)
        self._mark((e, self.cnt[e]), reads, writes)
        return ins

    def dma(self, e, out, in_, reads=(), writes=(), **kw):
        k = "d%d" % self.dma_rr
        self.dma_rr = (self.dma_rr + 1) % N_DMA_SEMS
        if self.cnt[k] > 0:
            self._wait(e, (k, self.cnt[k]))
        self._deps(e, reads, writes)
        ins = self.eng[e].dma_start(out=out, in_=in_, **kw)
        self.cnt[k] += 16
        ins.then_inc(self.sem[k], 16)
        self._mark((k, self.cnt[k]), reads, writes)
        return ins

    def finish(self):
        for i in range(N_DMA_SEMS):
            k = "d%d" % i
            if self.cnt[k]:
                self._wait("sp", (k, self.cnt[k]))
        for e in ("pe", "act", "dve", "pool"):
            if self.cnt[e]:
                self._wait("sp", (e, self.cnt[e]))
        self.es.close()


class StopBuild(Exception):
    pass


STOP = None


def ckpt(name):
    if STOP == name:
        raise StopBuild(name)


def build():
    kb = KB()
    nc = kb.nc
    EI = "ExternalInput"
    EO = "ExternalOutput"
    xs_d = kb.dram("xs", [2048, D], kind=EI)
    xp_d = kb.dram("xp", [512, D], kind=EI)
    ck_d = kb.dram("ck", [2, 512, 1024], kind=EI)
    cv_d = kb.dram("cv", [2, 512, 1024], kind=EI)
    st_d = kb.dram("st", [2, 2, 8, 64, 128], kind=EI)
    cvec_d = kb.dram("cvec", [2, D], kind=EI)
    wmod_d = kb.dram("w_mod", [2, D, 6 * D], kind=EI)
    bmod_d = kb.dram("b_mod", [2, 6 * D], kind=EI)
    win_d = kb.dram("w_in", [2, D, DIN], kind=EI)
    wout_d = kb.dram("w_out", [2, D, D], kind=EI)
    wup_d = kb.dram("w_up", [2, D, 2 * DFF], kind=EI)
    wdn_d = kb.dram("w_down", [2, DFF, D], kind=EI)
    ww_d = kb.dram("ww", [2, 16, 128, 9 * 128], kind=EI)
    colp_d = kb.dram("colp", [2, 128, 540], kind=EI)
    rowp_d = kb.dram("rowp", [2, 10272], kind=EI)
    ys_d = kb.dram("ys", [2048, D], kind=EO)
    yp_d = kb.dram("yp", [512, D], kind=EO)
    nk_d = kb.dram("nk", [2, 2, 256, 1024], kind=EO)
    nv_d = kb.dram("nv", [2, 2, 256, 1024], kind=EO)
    ns_d = kb.dram("ns", [2, 2, 2, 8, 64, 128], kind=EO)
    modd = kb.dram("modd", [2, 2, 6 * D])
    x1s_d = kb.dram("x1s", [2048, D])
    x1p_d = kb.dram("x1p", [512, D])
    x2s_d = kb.dram("x2s", [2048, D])
    x2p_d = kb.dram("x2p", [512, D])
    ccs_d = kb.dram("ccs", [128, 16, 2048], BF16)
    hs_d = kb.dram("hsd", [16, 128, 2, 512], BF16)
    dT = {}
    for nm, ap in (("modd", modd), ("x1s", x1s_d), ("x1p", x1p_d), ("x2s", x2s_d), ("x2p", x2p_d),
                   ("ys", ys_d), ("yp", yp_d), ("nk", nk_d), ("nv", nv_d), ("ns", ns_d), ("ccs", ccs_d), ("hs", hs_d)):
        dT[nm] = T(ap)

    def dump(name, t, ap, dt=F32):
        if STOP is None:
            return
        dd = kb.dram("dbg_" + name, list(ap.shape), dt, kind=EO)
        kb.dma("sp", dd, ap, reads=[t], writes=[T(dd)])

    ident = kb.sb([128, 128], F32)
    identb = kb.sb([128, 128], BF16)
    onesb = kb.sb([128, 128], BF16)
    onesf = kb.sb([128, 128], F32)
    LT = kb.sb([128, 128], F32)
    UT = kb.sb([128, 128], F32)
    kb.op("pool", lambda e: e.memset(ident.ap[:], 0.0), writes=[ident])
    kb.op("pool", lambda e: e.affine_select(out=ident.ap[:], in_=ident.ap[:], pattern=[[-1, 128]],
                                            compare_op=ALU.not_equal, fill=1.0, base=0, channel_multiplier=1),
          reads=[ident], writes=[ident])
    kb.op("dve", lambda e: e.tensor_copy(out=identb.ap[:], in_=ident.ap[:]), reads=[ident], writes=[identb])
    kb.op("pool", lambda e: e.memset(onesb.ap[:], 1.0), writes=[onesb])
    kb.op("pool", lambda e: e.memset(onesf.ap[:], 1.0), writes=[onesf])
    kb.op("pool", lambda e: e.memset(LT.ap[:], 1.0), writes=[LT])
    kb.op("pool", lambda e: e.memset(UT.ap[:], 1.0), writes=[UT])
    kb.op("pool", lambda e: e.affine_select(out=LT.ap[:], in_=LT.ap[:], pattern=[[1, 128]], compare_op=ALU.is_ge,
                                            fill=0.0, base=0, channel_multiplier=-1), reads=[LT], writes=[LT])
    kb.op("pool", lambda e: e.affine_select(out=UT.ap[:], in_=UT.ap[:], pattern=[[-1, 128]], compare_op=ALU.is_ge,
                                            fill=0.0, base=0, channel_multiplier=1), reads=[UT], writes=[UT])

    banks = [kb.ps([128, 512], F32) for _ in range(6)]
    kb_psb = [kb.ps([128, 512], BF16) for _ in range(2)]
    st8 = {"i": 0}
    pinned = set()

    def pbank():
        while True:
            b = banks[st8["i"] % 6]
            st8["i"] += 1
            if id(b) not in pinned:
                return b

    bigA = kb.sb([128, 32768], BF16)
    pools = {"w": [], "s": []}
    wst = {"i": 0}

    def alloc_pools(nw, ns_):
        pools["w"] = [kb.sb([128, 4096], BF16) for _ in range(nw)]
        pools["s"] = [kb.sb([128, 2048], F32) for _ in range(ns_)]

    def wbuf():
        b = pools["w"][wst["i"] % len(pools["w"])]
        wst["i"] += 1
        return b

    def dq():
        return "act"

    colp = kb.sb([128, 540], F32)
    modc = kb.sb([128, 4, 16], F32)
    rowS = kb.sb([128, 1568], F32)
    dexp = kb.sb([128, 512], F32)
    xres = kb.sb([128, D], F32)
    stats = kb.sb([128, 4, 6], F32)
    mv = kb.sb([128, 2], F32)
    rstd = kb.sb([128, 2], F32)
    rr = {"i": 0}

    def rot(lst):
        rr["i"] += 1
        return lst[rr["i"] % len(lst)]

    stg_rr = {"i": 0}
    defer = {"l": None}
    CAST_ENG = ("act", "dve")

    def stage_cast(wb, dst, src):
        a, b = dst.shape[1], dst.shape[2]
        stg = pools["s"][stg_rr["i"] % len(pools["s"])]
        ce = CAST_ENG[stg_rr["i"] % len(CAST_ENG)]
        stg_rr["i"] += 1
        sv = stg.ap[:, 0:a * b].rearrange("p (a b) -> p a b", a=a)
        kb.dma("sp", sv, src, writes=[stg])

        def do_cast():
            if ce == "act":
                kb.op("act", lambda e: e.activation(out=dst, in_=sv, func=AF.Identity), reads=[stg], writes=[wb])
            else:
                kb.op(ce, lambda e: e.tensor_copy(out=dst, in_=sv), reads=[stg], writes=[wb])
        if defer["l"] is None:
            do_cast()
        else:
            defer["l"].append(do_cast)

    def pipelined(n, load, compute, depth=1):
        loaded = {}
        for i in range(min(depth, n)):
            loaded[i] = load(i)
        for i in range(n):
            if i + depth < n:
                loaded[i + depth] = load(i + depth)
            compute(i, loaded.pop(i))

    def pipelined3(n, load, compute):
        res, pend = {}, {}

        def A(i):
            defer["l"] = []
            res[i] = load(i)
            pend[i] = defer["l"]
            defer["l"] = None

        def B(i):
            for c in pend.pop(i):
                c()
        A(0)
        if n > 1:
            A(1)
        B(0)
        for i in range(n):
            if i + 2 < n:
                A(i + 2)
            if i + 1 < n:
                B(i + 1)
            compute(i, res.pop(i))

    def load_w(src2d, kc, ncols):
        wb = wbuf()
        view = wb.ap[:, 0:kc * ncols].rearrange("p (c n) -> p c n", c=kc)
        srcv = src2d.rearrange("(c p) n -> p c n", p=128)
        step = max(1, 2048 // ncols)
        for c0 in range(0, kc, step):
            c1 = min(kc, c0 + step)
            stage_cast(wb, view[:, c0:c1, :], srcv[:, c0:c1, :])
        return wb, view

    def build_hT(dst, dview, xsrc, src_T, ntok, col0, sc_i, sh_i):
        xt = xres
        kb.dma(dq(), xt.ap[0:ntok, :], xsrc, reads=[src_T], writes=[xt])
        for q in range(4):
            pb = pbank()
            for kk in range(4):
                k = q * 4 + kk
                kb.op("pe", lambda e: e.transpose(pb.ap[:, kk * 128:kk * 128 + ntok], xt.ap[0:ntok, k * 128:(k + 1) * 128],
                                                  ident.ap[0:ntok, 0:ntok]), reads=[xt, ident], writes=[pb])
            for kk in range(4):
                k = q * 4 + kk
                kb.op("act", lambda e: e.activation(out=dview[:, k, col0:col0 + ntok], in_=pb.ap[:, kk * 128:kk * 128 + ntok],
                                                    func=AF.Identity, scale=modc.ap[:, sc_i, k:k + 1], bias=modc.ap[:, sh_i, k:k + 1]),
                      reads=[pb, modc], writes=[dst])

    def layer_norm_rows(src, sv, width, gtile, gview, btile, bview, out):
        nch = (width + 511) // 512
        for c in range(nch):
            kb.op("dve", lambda e: e.bn_stats(out=stats.ap[:, c, :], in_=sv[:, c * 512:min(width, (c + 1) * 512)]),
                  reads=[src], writes=[stats])
        kb.op("dve", lambda e: e.bn_aggr(out=mv.ap[:, :], in_=stats.ap[:, 0:nch, :]), reads=[stats], writes=[mv])
        kb.op("dve", lambda e: e.tensor_scalar_add(out=rstd.ap[:, 0:1], in0=mv.ap[:, 1:2], scalar1=EPS), reads=[mv], writes=[rstd])
        kb.op("act", lambda e: e.activation(out=rstd.ap[:, 0:1], in_=rstd.ap[:, 0:1], func=AF.Sqrt), reads=[rstd], writes=[rstd])
        kb.op("dve", lambda e: e.reciprocal(out=rstd.ap[:, 0:1], in_=rstd.ap[:, 0:1]), reads=[rstd], writes=[rstd])
        kb.op("dve", lambda e: e.scalar_tensor_tensor(out=rstd.ap[:, 1:2], in0=mv.ap[:, 0:1], scalar=-1.0, in1=rstd.ap[:, 0:1],
                                                      op0=ALU.mult, op1=ALU.mult), reads=[mv, rstd], writes=[rstd])
        kb.op("act", lambda e: e.activation(out=sv, in_=sv, func=AF.Identity,
                                            scale=rstd.ap[:, 0:1], bias=rstd.ap[:, 1:2]), reads=[src, rstd], writes=[src])
        kb.op("dve", lambda e: e.tensor_tensor(out=sv, in0=sv, in1=gview, op=ALU.mult),
              reads=[src, gtile], writes=[src])
        kb.op("pool", lambda e: e.tensor_tensor(out=out.ap[:, 0:width], in0=sv, in1=bview, op=ALU.add),
              reads=[src, btile], writes=[out])

    kb.push()
    vbuf = xres
    cT = kb.sb([128, 16, 2], F32)
    kb.dma("sp", vbuf.ap[0:2, :], cvec_d[:, :], writes=[vbuf])
    kb.op("act", lambda e: e.activation(out=vbuf.ap[0:2, :], in_=vbuf.ap[0:2, :], func=AF.Silu), reads=[vbuf], writes=[vbuf])
    pb = pbank()
    for k in range(16):
        kb.op("pe", lambda e: e.transpose(pb.ap[:, 2 * k:2 * k + 2], vbuf.ap[0:2, k * 128:(k + 1) * 128], ident.ap[0:2, 0:2]),
              reads=[vbuf, ident], writes=[pb])
    kb.op("dve", lambda e: e.tensor_copy(out=cT.ap[:].rearrange("p k c -> p (k c)"), in_=pb.ap[:, 0:32]), reads=[pb], writes=[cT])
    mrow = kb.sb([2, 512], F32)
    brow = kb.sb([2, 512], F32)
    bigAf = bigA.ap[:, :].bitcast(F32)
    wmT = [T(bigAf[:, 0:8192]), T(bigAf[:, 8192:16384])]
    for l in range(2):
        for g in range(24):
            wm = wmT[g % 2]
            wmv = wm.ap.rearrange("p (c n) -> p c n", c=16)
            for hf in range(2):
                kb.dma("act", wmv[:, hf * 8:(hf + 1) * 8, :],
                       wmod_d[l, hf * 1024:(hf + 1) * 1024, g * 512:(g + 1) * 512].rearrange("(c p) n -> p c n", p=128), writes=[wm])
            kb.dma("sp", brow.ap[:], bmod_d[l:l + 1, g * 512:(g + 1) * 512].partition_broadcast(2), writes=[brow])
            pb = pbank()
            for k in range(16):
                kb.op("pe", lambda e: e.matmul(pb.ap[0:2, :], lhsT=cT.ap[:, k, :], rhs=wmv[:, k, :], start=(k == 0), stop=(k == 15)),
                      reads=[cT, wm], writes=[pb])
            kb.op("dve", lambda e: e.tensor_tensor(out=mrow.ap[:], in0=pb.ap[0:2, :], in1=brow.ap[:], op=ALU.add),
                  reads=[pb, brow], writes=[mrow])
            kb.dma("sp", modd[l, :, g * 512:(g + 1) * 512], mrow.ap[:], reads=[mrow], writes=[dT["modd"]])
    kb.pop()

    def run_job(l, xin_d, xin_T, x1_d, x1_T, xout_d, xout_T, Tn, nseq, L, crow, sample):
        ntile = Tn // 128
        hT_v = bigA.ap[:, 0:16 * Tn].rearrange("p (k t) -> p k t", k=16)
        kb.dma("sp", colp.ap[:], colp_d[l, :, :], writes=[colp])
        for i, off in enumerate((D, 0, 4 * D, 3 * D)):
            kb.dma("sp", modc.ap[:, i, :], modd[l, crow, off:off + D].rearrange("(c p) -> p c", p=128),
                   reads=[dT["modd"]], writes=[modc], allow_slow_non_contiguous=True)
        for i in (0, 2):
            kb.op("dve", lambda e: e.tensor_scalar_add(out=modc.ap[:, i, :], in0=modc.ap[:, i, :], scalar1=1.0), reads=[modc], writes=[modc])
        kb.dma("sp", rowS.ap[:], rowp_d[l:l + 1, 8192:9760].partition_broadcast(128), writes=[rowS])
        kb.dma("sp", dexp.ap[:], rowp_d[l:l + 1, 9760:10272].partition_broadcast(128), writes=[dexp])
        kb.op("act", lambda e: e.activation(out=rowS.ap[:, 1536:1552], in_=rowS.ap[:, 1536:1552], func=AF.Exp), reads=[rowS], writes=[rowS])
        kb.op("dve", lambda e: e.tensor_scalar_mul(out=rowS.ap[:, 1536:1552], in0=rowS.ap[:, 1536:1552], scalar1=-1.0), reads=[rowS], writes=[rowS])

        for tt in range(ntile):
            build_hT(bigA, hT_v, xin_d[tt * 128:(tt + 1) * 128, :], xin_T, 128, tt * 128, 0, 1)

        wview_t = [None]

        def proj_fm(wview, c0, t0, n, pb):
            for k in range(16):
                kb.op("pe", lambda e: e.matmul(pb.ap[:, 0:n], lhsT=wview[:, k, c0:c0 + 128], rhs=hT_v[:, k, t0:t0 + n],
                                               start=(k == 0), stop=(k == 15)), reads=[bigA, wview_t[0]], writes=[pb])

        def proj_tm(wview, c0, ncols, tt, pb):
            for k in range(16):
                kb.op("pe", lambda e: e.matmul(pb.ap[:, 0:ncols], lhsT=hT_v[:, k, tt * 128:(tt + 1) * 128], rhs=wview[:, k, c0:c0 + ncols],
                                               start=(k == 0), stop=(k == 15)), reads=[bigA, wview_t[0]], writes=[pb])
        ckpt("A")

        kb.push()
        alloc_pools(6, 3)
        qT = kb.sb([128, Tn], BF16)
        kT = kb.sb([128, Tn], BF16)
        Vt = kb.sb([128, ntile, 128], BF16)
        ccst = kb.sb([128, Tn], BF16)
        rden = [kb.sb([128, 256], F32) for _ in range(2)]
        if sample:
            kcT = kb.sb([128, 512], BF16)
            Vc = kb.sb([128, 4, 128], BF16)
            ktmp = kb.sb([128, 4, 128], F32)
            wwt = kb.sb([128, 2, 9 * 128], BF16)
            Eb = [kb.sb([128, 1152], BF16) for _ in range(2)]
            Sb = [kb.sb([128, 640], F32) for _ in range(2)]
        else:
            Eb = [kb.sb([128, 512], BF16) for _ in range(2)]
            kvout = [kb.sb([128, 128], F32) for _ in range(2)]
        def att_load(hp):
            return (load_w(win_d[l, :, hp * 128:(hp + 1) * 128], 16, 128),
                    load_w(win_d[l, :, 1024 + hp * 128:1024 + (hp + 1) * 128], 16, 128),
                    load_w(win_d[l, :, 2048 + hp * 128:2048 + (hp + 1) * 128], 16, 128))

        def att_compute(hp, ws):
            (wq_t, wq), (wk_t, wk), (wv_t, wv) = ws
            for t0 in range(0, Tn, 512):
                pb = pbank()
                wview_t[0] = wq_t
                proj_fm(wq, 0, t0, 512, pb)
                kb.op("act", lambda e: e.activation(out=qT.ap[:, t0:t0 + 512], in_=pb.ap[:, :], func=AF.Identity, scale=0.125),
                      reads=[pb], writes=[qT])
                pb = pbank()
                wview_t[0] = wk_t
                proj_fm(wk, 0, t0, 512, pb)
                kb.op("dve", lambda e: e.tensor_copy(out=kT.ap[:, t0:t0 + 512], in_=pb.ap[:, :]), reads=[pb], writes=[kT])
            for tt in range(ntile):
                pb = pbank()
                wview_t[0] = wv_t
                proj_tm(wv, 0, 128, tt, pb)
                if sample:
                    kb.op("act", lambda e: e.activation(out=Vt.ap[:, tt, :], in_=pb.ap[:, 0:128], func=AF.Identity), reads=[pb], writes=[Vt])
                else:
                    s, tl = divmod(tt * 128, L)
                    ko = rot(kvout)
                    kb.op("dve", lambda e: e.tensor_copy(out=ko.ap[:], in_=pb.ap[:, 0:128]), reads=[pb], writes=[ko])
                    kb.op("act", lambda e: e.activation(out=Vt.ap[:, tt, :], in_=ko.ap[:], func=AF.Identity), reads=[ko], writes=[Vt])
                    kb.dma("sp", nv_d[s, l, tl:tl + 128, hp * 128:(hp + 1) * 128], ko.ap[:], reads=[ko], writes=[dT["nv"]])
                    pb2 = pbank()
                    wview_t[0] = wk_t
                    proj_tm(wk, 0, 128, tt, pb2)
                    ko2 = rot(kvout)
                    kb.op("dve", lambda e: e.tensor_copy(out=ko2.ap[:], in_=pb2.ap[:, 0:128]), reads=[pb2], writes=[ko2])
                    kb.dma("sp", nk_d[s, l, tl:tl + 128, hp * 128:(hp + 1) * 128], ko2.ap[:], reads=[ko2], writes=[dT["nk"]])
            if sample:
                kb.dma("sp", ktmp.ap[:], ck_d[l, :, hp * 128:(hp + 1) * 128].rearrange("(c p) n -> p c n", p=128), writes=[ktmp])
                pb = pbank()
                for c in range(4):
                    kb.op("pe", lambda e: e.transpose(pb.ap[:, c * 128:(c + 1) * 128], ktmp.ap[:, c, :], ident.ap[:]),
                          reads=[ktmp, ident], writes=[pb])
                kb.op("dve", lambda e: e.tensor_copy(out=kcT.ap[:], in_=pb.ap[:, :]), reads=[pb], writes=[kcT])
                stage_cast(Vc, Vc.ap[:, :, :], cv_d[l, :, hp * 128:(hp + 1) * 128].rearrange("(c p) n -> p c n", p=128))
                for hh in range(2):
                    wsrc = ww_d[l, 2 * hp + hh, :, :].rearrange("p (a b) -> p a b", a=9)
                    wdst = wwt.ap[:, hh, :].rearrange("p (a b) -> p a b", a=9)
                    stage_cast(wwt, wdst[:, 0:5, :], wsrc[:, 0:5, :])
                    stage_cast(wwt, wdst[:, 5:9, :], wsrc[:, 5:9, :])
            for hh in range(2):
                po = hh * 64
                if not sample:
                    for s in range(nseq):
                        b0 = s * L
                        E = rot(Eb)
                        pS = pbank()
                        for kc in range(2):
                            kb.op("pe", lambda e: e.matmul(pS.ap[:, kc * 256:(kc + 1) * 256], lhsT=kT.ap[po:po + 64, b0 + kc * 128:b0 + (kc + 1) * 128],
                                                           rhs=qT.ap[po:po + 64, b0:b0 + 256], start=True, stop=True), reads=[kT, qT], writes=[pS])
                        kb.op("act", lambda e: e.activation(out=E.ap[:, 0:512], in_=pS.ap[:, :], func=AF.Exp), reads=[pS], writes=[E])
                        pO = pbank()
                        pD = pbank()
                        for kc in range(2):
                            kb.op("pe", lambda e: e.matmul(pO.ap[:, 0:256], lhsT=Vt.ap[:, s * 2 + kc, :], rhs=E.ap[:, kc * 256:(kc + 1) * 256],
                                                           start=(kc == 0), stop=(kc == 1)), reads=[Vt, E], writes=[pO])
                        for kc in range(2):
                            kb.op("pe", lambda e: e.matmul(pD.ap[:, 0:256], lhsT=onesb.ap[:, :], rhs=E.ap[:, kc * 256:(kc + 1) * 256],
                                                           start=(kc == 0), stop=(kc == 1)), reads=[onesb, E], writes=[pD])
                        rd = rot(rden)
                        kb.op("dve", lambda e: e.reciprocal(out=rd.ap[po:po + 64, 0:256], in_=pD.ap[po:po + 64, 0:256]), reads=[pD], writes=[rd])
                        kb.op("dve", lambda e: e.tensor_tensor(out=ccst.ap[po:po + 64, b0:b0 + 256], in0=pO.ap[po:po + 64, 0:256],
                                                               in1=rd.ap[po:po + 64, 0:256], op=ALU.mult), reads=[pO, rd], writes=[ccst])
                else:
                    for i in range(16):
                        lo = min(max(i - 2, 0), 11)
                        d0 = lo - i + 4
                        pS1, pS2, pS3 = pbank(), pbank(), pbank()
                        for m in range(5):
                            dst = pS1.ap[:, m * 128:(m + 1) * 128] if m < 4 else pS2.ap[:, 0:128]
                            kb.op("pe", lambda e: e.matmul(dst, lhsT=kT.ap[po:po + 64, (lo + m) * 128:(lo + m + 1) * 128],
                                                           rhs=qT.ap[po:po + 64, i * 128:(i + 1) * 128], start=True, stop=True),
                                  reads=[kT, qT], writes=[pS1 if m < 4 else pS2])
                        for c in range(4):
                            kb.op("pe", lambda e: e.matmul(pS3.ap[:, c * 128:(c + 1) * 128], lhsT=kcT.ap[po:po + 64, c * 128:(c + 1) * 128],
                                                           rhs=qT.ap[po:po + 64, i * 128:(i + 1) * 128], start=True, stop=True),
                                  reads=[kcT, qT], writes=[pS3])
                        S = rot(Sb)
                        E = rot(Eb)
                        kb.op("dve", lambda e: e.tensor_tensor(out=S.ap[:, 0:512], in0=pS1.ap[:, :], in1=wwt.ap[:, hh, d0 * 128:(d0 + 4) * 128],
                                                               op=ALU.add), reads=[pS1, wwt], writes=[S])
                        kb.op("dve", lambda e: e.tensor_tensor(out=S.ap[:, 512:640], in0=pS2.ap[:, 0:128], in1=wwt.ap[:, hh, (d0 + 4) * 128:(d0 + 5) * 128],
                                                               op=ALU.add), reads=[pS2, wwt], writes=[S])
                        kb.op("act", lambda e: e.activation(out=E.ap[:, 0:640], in_=S.ap[:, :], func=AF.Exp), reads=[S], writes=[E])
                        kb.op("act", lambda e: e.activation(out=E.ap[:, 640:1152], in_=pS3.ap[:, :], func=AF.Exp), reads=[pS3], writes=[E])
                        pO, pD = pbank(), pbank()
                        for (pp, use_v) in ((pO, True), (pD, False)):
                            for c in range(4):
                                lh = Vc.ap[:, c, :] if use_v else onesb.ap[:, :]
                                kb.op("pe", lambda e: e.matmul(pp.ap[:, 0:128], lhsT=lh, rhs=E.ap[:, 640 + c * 128:640 + (c + 1) * 128],
                                                               start=(c == 0), stop=False), reads=[Vc, onesb, E], writes=[pp])
                            for b in range(2):
                                r = 2 * i + b
                                s0 = min(max(r - 4, 0), 24)
                                items = []
                                for m in range(5):
                                    aa = [a for a in range(2) if s0 <= 2 * (lo + m) + a < s0 + 8]
                                    if len(aa) == 2:
                                        items.append((m, 0, 128))
                                    elif len(aa) == 1:
                                        items.append((m, aa[0] * 64, aa[0] * 64 + 64))
                                for ii, (m, p0, p1) in enumerate(items):
                                    lh = Vt.ap[p0:p1, lo + m, :] if use_v else onesb.ap[p0:p1, :]
                                    kb.op("pe", lambda e: e.matmul(pp.ap[:, b * 64:(b + 1) * 64], lhsT=lh,
                                                                   rhs=E.ap[p0:p1, m * 128 + b * 64:m * 128 + (b + 1) * 64],
                                                                   start=False, stop=(b == 1 and ii == len(items) - 1)), reads=[Vt, onesb, E], writes=[pp])
                        rd = rot(rden)
                        kb.op("dve", lambda e: e.reciprocal(out=rd.ap[po:po + 64, 0:128], in_=pD.ap[po:po + 64, 0:128]), reads=[pD], writes=[rd])
                        kb.op("dve", lambda e: e.tensor_tensor(out=ccst.ap[po:po + 64, i * 128:(i + 1) * 128], in0=pO.ap[po:po + 64, 0:128],
                                                               in1=rd.ap[po:po + 64, 0:128], op=ALU.mult), reads=[pO, rd], writes=[ccst])
            kb.dma(dq(), ccs_d[:, hp, 0:Tn], ccst.ap[:, :], reads=[ccst], writes=[dT["ccs"]])
        pipelined(8, att_load, att_compute)
        kb.pop()
        ckpt("B1")

        kb.push()
        alloc_pools(4, 5)
        PADC = 15
        Lp = L + 2 * PADC
        glu = kb.sb([128, nseq * Lp], F32)
        cacc = xres
        sig = kb.sb([128, 512], F32)
        tm512 = kb.sb([128, 512], F32)
        tmb = kb.sb([128, 512], BF16)
        cstage = kb.sb([128, ntile, 512], BF16)
        cc4 = [kb.sb([128, 4, 128], BF16) for _ in range(2)]
        gl_v = glu.ap[:, :].rearrange("p (s t) -> p s t", s=nseq)
        ca_v = cacc.ap[:, 0:Tn].rearrange("p (s t) -> p s t", s=nseq)
        def conf_load(c):
            return (load_w(win_d[l, :, 3072 + c * 128:3072 + (c + 1) * 128], 16, 128),
                    load_w(win_d[l, :, 3584 + c * 128:3584 + (c + 1) * 128], 16, 128))

        def conf_compute(c, ws):
            (wu_t, wu), (wg_t, wg) = ws
            kb.op("pool", lambda e: e.memset(glu.ap[:, :], 0.0), writes=[glu])
            for s in range(nseq):
                for t0 in range(0, L, 512):
                    n = min(512, L - t0)
                    pu, pg = pbank(), pbank()
                    wview_t[0] = wu_t
                    proj_fm(wu, 0, s * L + t0, n, pu)
                    wview_t[0] = wg_t
                    proj_fm(wg, 0, s * L + t0, n, pg)
                    kb.op("act", lambda e: e.activation(out=sig.ap[:, 0:n], in_=pg.ap[:, 0:n], func=AF.Sigmoid), reads=[pg], writes=[sig])
                    kb.op("dve", lambda e: e.tensor_tensor(out=gl_v[:, s, PADC + t0:PADC + t0 + n], in0=pu.ap[:, 0:n], in1=sig.ap[:, 0:n],
                                                           op=ALU.mult), reads=[pu, sig], writes=[glu])
            kb.op("dve", lambda e: e.tensor_scalar(out=ca_v, in0=gl_v[:, :, 0:L], scalar1=colp.ap[:, c * 31:c * 31 + 1],
                                                   scalar2=colp.ap[:, 124 + c:125 + c], op0=ALU.mult, op1=ALU.add), reads=[glu, colp], writes=[cacc])
            for k in range(1, 31):
                kb.op("dve", lambda e: e.scalar_tensor_tensor(out=ca_v, in0=gl_v[:, :, k:k + L], scalar=colp.ap[:, c * 31 + k:c * 31 + k + 1],
                                                              in1=ca_v, op0=ALU.mult, op1=ALU.add), reads=[glu, colp, cacc], writes=[cacc])
            for tt in range(ntile):
                pb = pbank()
                kb.op("pe", lambda e: e.transpose(pb.ap[:, 0:128], cacc.ap[:, tt * 128:(tt + 1) * 128], ident.ap[:]), reads=[cacc, ident], writes=[pb])
                kb.op("act", lambda e: e.activation(out=cstage.ap[:, tt, c * 128:(c + 1) * 128], in_=pb.ap[:, 0:128], func=AF.Identity), reads=[pb], writes=[cstage])
        pipelined(4, conf_load, conf_compute)
        for tt in range(ntile):
            kb.op("dve", lambda e: e.tensor_copy(out=tm512.ap[:], in_=cstage.ap[:, tt, :]), reads=[cstage], writes=[tm512])
            layer_norm_rows(tm512, tm512.ap[:, :], 512, rowS, rowS.ap[:, 0:512], rowS, rowS.ap[:, 512:1024], tm512)
            kb.op("act", lambda e: e.activation(out=tmb.ap[:], in_=tm512.ap[:], func=AF.Silu), reads=[tm512], writes=[tmb])
            pbt = kb_psb[tt % 2]
            for c in range(4):
                kb.op("pe", lambda e: e.transpose(pbt.ap[:, c * 128:(c + 1) * 128], tmb.ap[:, c * 128:(c + 1) * 128], identb.ap[:]),
                      reads=[tmb, identb], writes=[pbt])
            c4 = cc4[tt % 2]
            kb.op("dve", lambda e: e.tensor_copy(out=c4.ap[:].rearrange("p a b -> p (a b)"), in_=pbt.ap[:, 0:512]), reads=[pbt], writes=[c4])
            kb.dma(dq(), ccs_d[:, 8:12, tt * 128:(tt + 1) * 128], c4.ap[:], reads=[c4], writes=[dT["ccs"]])
        kb.pop()
        ckpt("B2")

        kb.push()
        alloc_pools(2, 2) if sample else alloc_pools(4, 4)
        PADS = 2
        Ls = L + 2 * PADS
        xcT = kb.sb([128, 8, Tn], BF16)
        xraw = kb.sb([128, nseq * Ls], F32)
        cacc = xres
        ca_v = cacc.ap[:, 0:Tn].rearrange("p (s t) -> p s t", s=nseq)
        dtt = kb.sb([128, ntile, 16], F32)
        dta = kb.sb([128, ntile, 16], F32)
        cs = kb.sb([128, ntile, 16], F32)
        ncs = kb.sb([128, ntile, 16], F32)
        ecs = kb.sb([128, ntile, 16], F32)
        dend = kb.sb([128, ntile, 16], F32)
        cdec = kb.sb([128, ntile, 16], F32)
        tot = kb.sb([128, 16], F32)
        xsTt = [kb.sb([128, 512], BF16) for _ in range(2)]
        Btt = [kb.sb([128, 256], BF16) for _ in range(2)]
        xdtt = [kb.sb([128, 2, 512], BF16) for _ in range(2)]
        xdw = [kb.sb([128, 512], BF16) for _ in range(2)]
        zst = [kb.sb([128, 512], BF16) for _ in range(2)]
        HT = kb.sb([128, 2, 512], F32)
        Hst = [kb.sb([128, 2, 512], BF16) for _ in range(2)]
        Gm = [kb.sb([128, 4, 128], F32) for _ in range(2)]
        A1 = [kb.sb([128, 128], F32) for _ in range(2)]
        Dm = [kb.sb([128, 128], F32) for _ in range(2)]
        Ms = [kb.sb([128, 128], BF16) for _ in range(2)]
        ydsb = kb.sb([128, 512], F32)
        ysb = kb.sb([128, 512], F32)
        sttmp = kb.sb([64, 128], F32)
        ssq = kb.sb([128, 2], F32)
        tmb = kb.sb([128, 512], BF16)
        cc4 = [kb.sb([128, 4, 128], BF16) for _ in range(2)]
        xr_v = xraw.ap[:, :].rearrange("p (s t) -> p s t", s=nseq)
        def xbc_load(c):
            return load_w(win_d[l, :, 4608 + c * 128:4608 + (c + 1) * 128], 16, 128)

        def xbc_compute(c, ws):
            wx_t, wx = ws
            kb.op("pool", lambda e: e.memset(xraw.ap[:, :], 0.0), writes=[xraw])
            for s in range(nseq):
                for t0 in range(0, L, 512):
                    n = min(512, L - t0)
                    pb = pbank()
                    wview_t[0] = wx_t
                    proj_fm(wx, 0, s * L + t0, n, pb)
                    kb.op("act", lambda e: e.activation(out=xr_v[:, s, PADS + t0:PADS + t0 + n], in_=pb.ap[:, 0:n], func=AF.Identity), reads=[pb], writes=[xraw])
            kb.op("dve", lambda e: e.tensor_scalar(out=ca_v, in0=xr_v[:, :, 0:L], scalar1=colp.ap[:, 128 + c * 5:129 + c * 5],
                                                   scalar2=colp.ap[:, 168 + c:169 + c], op0=ALU.mult, op1=ALU.add), reads=[xraw, colp], writes=[cacc])
            for k in range(1, 5):
                kb.op("dve", lambda e: e.scalar_tensor_tensor(out=ca_v, in0=xr_v[:, :, k:k + L], scalar=colp.ap[:, 128 + c * 5 + k:129 + c * 5 + k],
                                                              in1=ca_v, op0=ALU.mult, op1=ALU.add), reads=[xraw, colp, cacc], writes=[cacc])
            kb.op("act", lambda e: e.activation(out=xcT.ap[:, c, :], in_=cacc.ap[:, 0:Tn], func=AF.Silu), reads=[cacc], writes=[xcT])
        pipelined(8, xbc_load, xbc_compute)
        wd_t, wdv = load_w(win_d[l, :, 5632:5648], 16, 16)
        for tt in range(ntile):
            pb = pbank()
            wview_t[0] = wd_t
            proj_tm(wdv, 0, 16, tt, pb)
            kb.op("dve", lambda e: e.tensor_tensor(out=dtt.ap[:, tt, :], in0=pb.ap[:, 0:16], in1=rowS.ap[:, 1552:1568], op=ALU.add),
                  reads=[pb, rowS], writes=[dtt])
        kb.op("act", lambda e: e.activation(out=dtt.ap[:], in_=dtt.ap[:], func=AF.Exp), reads=[dtt], writes=[dtt])
        kb.op("act", lambda e: e.activation(out=dtt.ap[:], in_=dtt.ap[:], func=AF.Ln, bias=1.0), reads=[dtt], writes=[dtt])
        for tt in range(ntile):
            kb.op("dve", lambda e: e.tensor_tensor(out=dta.ap[:, tt, :], in0=dtt.ap[:, tt, :], in1=rowS.ap[:, 1536:1552], op=ALU.mult),
                  reads=[dtt, rowS], writes=[dta])
        for tt in range(ntile):
            pb = pbank()
            kb.op("pe", lambda e: e.matmul(pb.ap[:, 0:8], lhsT=LT.ap[:], rhs=dta.ap[:, tt, 0:8], start=True, stop=True), reads=[LT, dta], writes=[pb])
            kb.op("pe", lambda e: e.matmul(pb.ap[:, 8:16], lhsT=UT.ap[:], rhs=dta.ap[:, tt, 8:16], start=True, stop=True), reads=[UT, dta], writes=[pb])
            kb.op("pe", lambda e: e.matmul(pb.ap[:, 16:32], lhsT=onesf.ap[:], rhs=dta.ap[:, tt, :], start=True, stop=True), reads=[onesf, dta], writes=[pb])
            kb.op("dve", lambda e: e.tensor_copy(out=cs.ap[:, tt, :], in_=pb.ap[:, 0:16]), reads=[pb], writes=[cs])
            kb.op("dve", lambda e: e.tensor_copy(out=tot.ap[:], in_=pb.ap[:, 16:32]), reads=[pb], writes=[tot])
            kb.op("dve", lambda e: e.tensor_scalar_mul(out=ncs.ap[:, tt, :], in0=cs.ap[:, tt, :], scalar1=-1.0), reads=[cs], writes=[ncs])
            kb.op("act", lambda e: e.activation(out=ecs.ap[:, tt, :], in_=cs.ap[:, tt, :], func=AF.Exp), reads=[cs], writes=[ecs])
            kb.op("act", lambda e: e.activation(out=cdec.ap[:, tt, :], in_=tot.ap[:], func=AF.Exp), reads=[tot], writes=[cdec])
            kb.op("dve", lambda e: e.tensor_tensor(out=dend.ap[:, tt, :], in0=tot.ap[:], in1=cs.ap[:, tt, :], op=ALU.subtract),
                  reads=[tot, cs], writes=[dend])
        kb.op("act", lambda e: e.activation(out=dend.ap[:], in_=dend.ap[:], func=AF.Exp), reads=[dend], writes=[dend])

        def tok_major(tt, need_b):
            xs_ = rot(xsTt)
            pbt = kb_psb[0]
            for c in range(4):
                kb.op("pe", lambda e: e.transpose(pbt.ap[:, c * 128:(c + 1) * 128], xcT.ap[:, c, tt * 128:(tt + 1) * 128], identb.ap[:]),
                      reads=[xcT, identb], writes=[pbt])
            kb.op("act", lambda e: e.activation(out=xs_.ap[:], in_=pbt.ap[:, 0:512], func=AF.Identity), reads=[pbt], writes=[xs_])
            b_ = None
            if need_b:
                b_ = rot(Btt)
                pbt2 = kb_psb[1]
                for c in range(2):
                    kb.op("pe", lambda e: e.transpose(pbt2.ap[:, c * 128:(c + 1) * 128], xcT.ap[:, 4 + c, tt * 128:(tt + 1) * 128], identb.ap[:]),
                          reads=[xcT, identb], writes=[pbt2])
                kb.op("act", lambda e: e.activation(out=b_.ap[:], in_=pbt2.ap[:, 0:256], func=AF.Identity), reads=[pbt2], writes=[b_])
            return xs_, b_

        def mk_xdt(xs_, tt, dr, dst_ap, dst_t):
            kb.op("dve", lambda e: e.tensor_tensor(out=dst_ap.rearrange("p (h q) -> p h q", h=8),
                                                   in0=xs_.ap[:].rearrange("p (h q) -> p h q", h=8),
                                                   in1=dtt.ap[:, tt, dr * 8:(dr + 1) * 8].unsqueeze(2).to_broadcast([128, 8, 64]), op=ALU.mult),
                  reads=[xs_, dtt], writes=[dst_t])

        nch = L // 128
        for s in range(nseq):
            if sample:
                for dr in range(2):
                    for h in range(8):
                        kb.dma(dq(), sttmp.ap[:], st_d[l, dr, h, :, :], writes=[sttmp])
                        pb = pbank()
                        kb.op("pe", lambda e: e.transpose(pb.ap[:, 0:64], sttmp.ap[:, :], ident.ap[0:64, 0:64]), reads=[sttmp, ident], writes=[pb])
                        kb.op("dve", lambda e: e.tensor_copy(out=HT.ap[:, dr, h * 64:(h + 1) * 64], in_=pb.ap[:, 0:64]), reads=[pb], writes=[HT])
            else:
                kb.op("pool", lambda e: e.memset(HT.ap[:], 0.0), writes=[HT])
            for dr in range(2):
                order = range(nch) if dr == 0 else range(nch - 1, -1, -1)
                for cch in order:
                    tt = s * nch + cch
                    hst = rot(Hst)
                    kb.op("act", lambda e: e.activation(out=hst.ap[:, dr, :], in_=HT.ap[:, dr, :], func=AF.Identity), reads=[HT], writes=[hst])
                    kb.dma(dq(), hs_d[cch, :, dr, :], hst.ap[:, dr, :], reads=[hst], writes=[dT["hs"]])
                    xs_, b_ = tok_major(tt, True)
                    xd = rot(xdtt)
                    mk_xdt(xs_, tt, dr, xd.ap[:, 0, :], xd)
                    xw = rot(xdw)
                    kb.op("dve", lambda e: e.tensor_tensor(out=xw.ap[:, :].rearrange("p (h q) -> p h q", h=8),
                                                           in0=xd.ap[:, 0, :].rearrange("p (h q) -> p h q", h=8),
                                                           in1=dend.ap[:, tt, dr * 8:(dr + 1) * 8].unsqueeze(2).to_broadcast([128, 8, 64]), op=ALU.mult),
                          reads=[xd, dend], writes=[xw])
                    pb = pbank()
                    for g in range(2):
                        kb.op("pe", lambda e: e.matmul(pb.ap[:, g * 256:(g + 1) * 256], lhsT=b_.ap[:, g * 128:(g + 1) * 128],
                                                       rhs=xw.ap[:, g * 256:(g + 1) * 256], start=True, stop=True), reads=[b_, xw], writes=[pb])
                    kb.op("dve", lambda e: e.tensor_tensor(out=HT.ap[:, dr, :].rearrange("p (h q) -> p h q", h=8),
                                                           in0=HT.ap[:, dr, :].rearrange("p (h q) -> p h q", h=8),
                                                           in1=cdec.ap[:, tt, dr * 8:(dr + 1) * 8].unsqueeze(2).to_broadcast([128, 8, 64]), op=ALU.mult),
                          reads=[HT, cdec], writes=[HT])
                    kb.op("dve", lambda e: e.tensor_tensor(out=HT.ap[:, dr, :], in0=HT.ap[:, dr, :], in1=pb.ap[:, :], op=ALU.add),
                          reads=[HT, pb], writes=[HT])
                if not sample:
                    for h in range(8):
                        pb = pbank()
                        kb.op("pe", lambda e: e.transpose(pb.ap[0:64, 0:128], HT.ap[:, dr, h * 64:(h + 1) * 64], ident.ap[:]), reads=[HT, ident], writes=[pb])
                        kb.op("dve", lambda e: e.tensor_copy(out=sttmp.ap[:], in_=pb.ap[0:64, 0:128]), reads=[pb], writes=[sttmp])
                        kb.dma(dq(), ns_d[s, l, dr, h, :, :], sttmp.ap[:], reads=[sttmp], writes=[dT["ns"]])
            wz_t, wz = load_w(win_d[l, :, 4096:4352], 16, 256)
            wz2_t, wz2 = load_w(win_d[l, :, 4352:4608], 16, 256)
            for cch in range(nch):
                tt = s * nch + cch
                tok0 = tt * 128
                xs_, _ = tok_major(tt, False)
                xd = rot(xdtt)
                mk_xdt(xs_, tt, 0, xd.ap[:, 0, :], xd)
                mk_xdt(xs_, tt, 1, xd.ap[:, 1, :], xd)
                zt = rot(zst)
                for (wzt_, wzv_, c0) in ((wz_t, wz, 0), (wz2_t, wz2, 256)):
                    pb = pbank()
                    wview_t[0] = wzt_
                    proj_tm(wzv_, 0, 256, tt, pb)
                    kb.op("act", lambda e: e.activation(out=zt.ap[:, c0:c0 + 256], in_=pb.ap[:, 0:256], func=AF.Silu), reads=[pb], writes=[zt])
                hst = rot(Hst)
                kb.dma(dq(), hst.ap[:], hs_d[cch, :, :, :], reads=[dT["hs"]], writes=[hst])
                gm = rot(Gm)
                for g in range(2):
                    pb = pbank()
                    kb.op("pe", lambda e: e.matmul(pb.ap[:, 0:128], lhsT=xcT.ap[:, 4 + g, tok0:tok0 + 128], rhs=xcT.ap[:, 6 + g, tok0:tok0 + 128],
                                                   start=True, stop=True), reads=[xcT], writes=[pb])
                    kb.op("dve", lambda e: e.tensor_tensor(out=gm.ap[:, g * 2, :], in0=pb.ap[:, 0:128], in1=LT.ap[:], op=ALU.mult), reads=[pb, LT], writes=[gm])
                    kb.op("dve", lambda e: e.tensor_tensor(out=gm.ap[:, g * 2 + 1, :], in0=pb.ap[:, 0:128], in1=UT.ap[:], op=ALU.mult), reads=[pb, UT], writes=[gm])
                pYd, pYf, pYb = pbank(), pbank(), pbank()
                pinned.update((id(pYd), id(pYf), id(pYb)))
                for h in range(8):
                    g = h // 4
                    for dr in range(2):
                        col = dr * 8 + h
                        a1, dm, ms = rot(A1), rot(Dm), rot(Ms)
                        kb.op("pool", lambda e: e.tensor_scalar_mul(out=a1.ap[:], in0=onesf.ap[:], scalar1=dta.ap[:, tt, col:col + 1]),
                              reads=[onesf, dta], writes=[a1])
                        pb = pbank()
                        kb.op("pe", lambda e: e.matmul(pb.ap[:, 0:128], lhsT=a1.ap[:], rhs=(LT if dr == 0 else UT).ap[:], start=True, stop=True),
                              reads=[a1, LT, UT], writes=[pb])
                        kb.op("dve", lambda e: e.tensor_scalar(out=dm.ap[:], in0=pb.ap[:, 0:128], scalar1=ncs.ap[:, tt, col:col + 1], scalar2=0.0,
                                                               op0=ALU.add, op1=ALU.min), reads=[pb, ncs], writes=[dm])
                        kb.op("act", lambda e: e.activation(out=dm.ap[:], in_=dm.ap[:], func=AF.Exp), reads=[dm], writes=[dm])
                        kb.op("dve", lambda e: e.tensor_tensor(out=ms.ap[:], in0=dm.ap[:], in1=gm.ap[:, g * 2 + dr, :], op=ALU.mult),
                              reads=[dm, gm], writes=[ms])
                        kb.op("pe", lambda e: e.matmul(pYd.ap[:, h * 64:(h + 1) * 64], lhsT=ms.ap[:], rhs=xd.ap[:, dr, h * 64:(h + 1) * 64],
                                                       start=(dr == 0), stop=(dr == 1)), reads=[ms, xd], writes=[pYd])
                        pY = pYf if dr == 0 else pYb
                        kb.op("pe", lambda e: e.matmul(pY.ap[:, h * 64:(h + 1) * 64], lhsT=xcT.ap[:, 6 + g, tok0:tok0 + 128],
                                                       rhs=hst.ap[:, dr, h * 64:(h + 1) * 64], start=True, stop=True), reads=[xcT, hst], writes=[pY])
                pinned.clear()
                kb.op("act", lambda e: e.activation(out=ydsb.ap[:], in_=pYd.ap[:, :], func=AF.Identity), reads=[pYd], writes=[ydsb])
                for dr, pY in ((0, pYf), (1, pYb)):
                    kb.op("dve", lambda e: e.tensor_tensor(out=ysb.ap[:].rearrange("p (h q) -> p h q", h=8),
                                                           in0=pY.ap[:, :].rearrange("p (h q) -> p h q", h=8),
                                                           in1=ecs.ap[:, tt, dr * 8:(dr + 1) * 8].unsqueeze(2).to_broadcast([128, 8, 64]), op=ALU.mult),
                          reads=[pY, ecs], writes=[ysb])
                    kb.op("dve", lambda e: e.tensor_tensor(out=ydsb.ap[:], in0=ydsb.ap[:], in1=ysb.ap[:], op=ALU.add), reads=[ydsb, ysb], writes=[ydsb])
                kb.op("dve", lambda e: e.tensor_tensor(out=ysb.ap[:], in0=xs_.ap[:], in1=dexp.ap[:], op=ALU.mult), reads=[xs_, dexp], writes=[ysb])
                kb.op("dve", lambda e: e.tensor_tensor(out=ydsb.ap[:], in0=ydsb.ap[:], in1=ysb.ap[:], op=ALU.add), reads=[ydsb, ysb], writes=[ydsb])
                kb.op("dve", lambda e: e.tensor_tensor(out=ydsb.ap[:], in0=ydsb.ap[:], in1=zt.ap[:], op=ALU.mult), reads=[ydsb, zt], writes=[ydsb])
                kb.op("act", lambda e: e.activation(out=ysb.ap[:], in_=ydsb.ap[:], func=AF.Square, accum_out=ssq.ap[:, 0:1]), reads=[ydsb], writes=[ysb, ssq])
                kb.op("dve", lambda e: e.tensor_scalar(out=ssq.ap[:, 1:2], in0=ssq.ap[:, 0:1], scalar1=1.0 / 512, scalar2=EPS, op0=ALU.mult, op1=ALU.add),
                      reads=[ssq], writes=[ssq])
                kb.op("act", lambda e: e.activation(out=ssq.ap[:, 1:2], in_=ssq.ap[:, 1:2], func=AF.Sqrt), reads=[ssq], writes=[ssq])
                kb.op("dve", lambda e: e.reciprocal(out=ssq.ap[:, 1:2], in_=ssq.ap[:, 1:2]), reads=[ssq], writes=[ssq])
                kb.op("dve", lambda e: e.scalar_tensor_tensor(out=tmb.ap[:], in0=ydsb.ap[:], scalar=ssq.ap[:, 1:2], in1=rowS.ap[:, 1024:1536],
                                                              op0=ALU.mult, op1=ALU.mult), reads=[ydsb, ssq, rowS], writes=[tmb])
                pbt = kb_psb[1]
                for c in range(4):
                    kb.op("pe", lambda e: e.transpose(pbt.ap[:, c * 128:(c + 1) * 128], tmb.ap[:, c * 128:(c + 1) * 128], identb.ap[:]),
                          reads=[tmb, identb], writes=[pbt])
                c4 = cc4[tt % 2]
                kb.op("act", lambda e: e.activation(out=c4.ap[:].rearrange("p a b -> p (a b)"), in_=pbt.ap[:, 0:512], func=AF.Identity),
                      reads=[pbt], writes=[c4])
                kb.dma(dq(), ccs_d[:, 12:16, tok0:tok0 + 128], c4.ap[:], reads=[c4], writes=[dT["ccs"]])
        kb.pop()
        ckpt("B3")

        kb.push()
        bigB = kb.sb([128, 16384], BF16)
        growA = kb.sb([128, D], F32)
        lngA = kb.sb([128, D], F32)
        lnbA = kb.sb([128, D], F32)
        raws = [kb.sb([128, 520], F32) for _ in range(4)]
        kb.dma("sp", growA.ap[:], modd[l, crow:crow + 1, 2 * D:3 * D].partition_broadcast(128), reads=[dT["modd"]], writes=[growA])
        kb.dma("sp", lngA.ap[:], rowp_d[l:l + 1, 0:D].partition_broadcast(128), writes=[lngA])
        kb.dma("sp", lnbA.ap[:], rowp_d[l:l + 1, D:2 * D].partition_broadcast(128), writes=[lnbA])
        kb.push()
        alloc_pools(4, 3)
        pre_v = bigA.ap[:, 0:16384].bitcast(F32).rearrange("p (a n) -> p a n", a=4)
        ccg_v = bigB.ap[:, 0:8192].rearrange("p (k t) -> p k t", k=16)
        for g0 in range(0, Tn, 512):
            kb.dma(dq(), ccg_v, ccs_d[:, :, g0:g0 + 512], reads=[dT["ccs"]], writes=[bigB])
            def c_load(cg):
                wts = []
                for hf in range(2):
                    wb = wbuf()
                    wv_ = wb.ap[:, 0:2048].rearrange("p (c n) -> p c n", c=8)
                    stage_cast(wb, wv_, wout_d[l, hf * 1024:(hf + 1) * 1024, cg * 256:(cg + 1) * 256].rearrange("(c p) n -> p c n", p=128))
                    wts.append((wb, wv_))
                return wts

            def c_compute(cg, wts):
                for ti in range(4):
                    pb = pbank()
                    for k in range(16):
                        wb, wv_ = wts[k // 8]
                        kb.op("pe", lambda e: e.matmul(pb.ap[:, 0:256], lhsT=ccg_v[:, k, ti * 128:(ti + 1) * 128], rhs=wv_[:, k % 8, :],
                                                       start=(k == 0), stop=(k == 15)), reads=[bigB, wb], writes=[pb])
                    kb.op("dve", lambda e: e.tensor_tensor(out=pre_v[:, ti, cg * 256:(cg + 1) * 256], in0=pb.ap[:, 0:256], in1=growA.ap[:, cg * 256:(cg + 1) * 256],
                                                           op=ALU.mult), reads=[pb, growA], writes=[bigA])
            pipelined(8, c_load, c_compute)
            for ti in range(4):
                r0 = g0 + ti * 128
                kb.dma(dq(), xres.ap[:], xin_d[r0:r0 + 128, :], reads=[xin_T], writes=[xres])
                kb.op("dve", lambda e: e.scalar_tensor_tensor(out=pre_v[:, ti, :], in0=xres.ap[:], scalar=ALPHA, in1=pre_v[:, ti, :], op0=ALU.mult, op1=ALU.add),
                      reads=[xres, bigA], writes=[bigA])
                layer_norm_rows(bigA, pre_v[:, ti, :], D, lngA, lngA.ap[:], lnbA, lnbA.ap[:], xres)
                kb.dma(dq(), x1_d[r0:r0 + 128, :], xres.ap[:], reads=[xres], writes=[x1_T])
        kb.pop()
        ckpt("C")

        kb.dma("sp", growA.ap[:], modd[l, crow:crow + 1, 5 * D:6 * D].partition_broadcast(128), reads=[dT["modd"]], writes=[growA])
        kb.dma("sp", lngA.ap[:], rowp_d[l:l + 1, 2 * D:3 * D].partition_broadcast(128), writes=[lngA])
        kb.dma("sp", lnbA.ap[:], rowp_d[l:l + 1, 3 * D:4 * D].partition_broadcast(128), writes=[lnbA])
        TG = 512
        W2 = TG + 2
        nseg = 1 if sample else 2
        SL = TG // nseg
        RW = nseg * (SL + 2)
        h2_v = bigA.ap[:, 0:16 * W2].rearrange("p (k t) -> p k t", k=16)
        act_v = bigA.ap[:, 16 * W2:16 * W2 + 44 * TG].rearrange("p (k t) -> p k t", k=44)
        h2T = T(bigA.ap[:, 0:16 * W2])
        actT = T(bigA.ap[:, 16 * W2:16 * W2 + 44 * TG])
        ff_v = bigB.ap[:, :].bitcast(F32).rearrange("p (a n) -> p a n", a=4)
        ra_t, rg_t, aa_t, ag_t = raws
        for g0 in range(0, Tn, TG):
            lh = sample and g0 > 0
            rh = sample and (g0 + TG < Tn)
            kb.op("pool", lambda e: e.memset(bigA.ap[:, 0:16 * W2], 0.0), writes=[h2T])
            for ti in range(TG // 128):
                build_hT(h2T, h2_v, x1_d[g0 + ti * 128:g0 + (ti + 1) * 128, :], x1_T, 128, 2 + ti * 128, 2, 3)
            if lh:
                build_hT(h2T, h2_v, x1_d[g0 - 1:g0, :], x1_T, 1, 0, 2, 3)
            if rh:
                build_hT(h2T, h2_v, x1_d[g0 + TG:g0 + TG + 1, :], x1_T, 1, 1, 2, 3)
            kb.op("pool", lambda e: e.memset(ra_t.ap[:, 0:RW], 0.0), writes=[ra_t])
            kb.op("pool", lambda e: e.memset(rg_t.ap[:, 0:RW], 0.0), writes=[rg_t])
            kb.push()
            alloc_pools(3, 4)
            def up_load(j):
                wb = wbuf()
                wa = wb.ap[:, 0:2048].rearrange("p (c n) -> p c n", c=16)
                wg = wb.ap[:, 2048:4096].rearrange("p (c n) -> p c n", c=16)
                stage_cast(wb, wa, wup_d[l, :, j * 128:(j + 1) * 128].rearrange("(c p) n -> p c n", p=128))
                stage_cast(wb, wg, wup_d[l, :, DFF + j * 128:DFF + (j + 1) * 128].rearrange("(c p) n -> p c n", p=128))
                return wb, wa, wg

            def up_compute(j, ws):
                wb, wa, wg = ws
                for (wv_, rt) in ((wa, ra_t), (wg, rg_t)):
                    pb = pbank()
                    for k in range(16):
                        kb.op("pe", lambda e: e.matmul(pb.ap[:, 0:TG], lhsT=wv_[:, k, :], rhs=h2_v[:, k, 2:TG + 2], start=(k == 0), stop=(k == 15)),
                              reads=[wb, h2T], writes=[pb])
                    kb.op("act", lambda e: e.activation(out=rt.ap[:, 0:RW].rearrange("p (s t) -> p s t", s=nseg)[:, :, 1:SL + 1],
                                                        in_=pb.ap[:, 0:TG].rearrange("p (s t) -> p s t", s=nseg), func=AF.Identity), reads=[pb], writes=[rt])
                    if lh or rh:
                        pb = pbank()
                        for k in range(16):
                            kb.op("pe", lambda e: e.matmul(pb.ap[:, 0:2], lhsT=wv_[:, k, :], rhs=h2_v[:, k, 0:2],
                                                           start=(k == 0), stop=(k == 15)), reads=[wb, h2T], writes=[pb])
                        if lh:
                            kb.op("act", lambda e: e.activation(out=rt.ap[:, 0:1], in_=pb.ap[:, 0:1], func=AF.Identity), reads=[pb], writes=[rt])
                        if rh:
                            kb.op("act", lambda e: e.activation(out=rt.ap[:, TG + 1:TG + 2], in_=pb.ap[:, 1:2], func=AF.Identity), reads=[pb], writes=[rt])
                for (rt, at, cj) in ((ra_t, aa_t, j), (rg_t, ag_t, 44 + j)):
                    rv3 = rt.ap[:, 0:RW].rearrange("p (s t) -> p s t", s=nseg)
                    av3 = at.ap[:, 0:TG].rearrange("p (s t) -> p s t", s=nseg)
                    kb.op("dve", lambda e: e.tensor_scalar(out=av3, in0=rv3[:, :, 0:SL], scalar1=colp.ap[:, 176 + cj * 3:177 + cj * 3], scalar2=None,
                                                           op0=ALU.mult), reads=[rt, colp], writes=[at])
                    for k in (1, 2):
                        kb.op("dve", lambda e: e.scalar_tensor_tensor(out=av3, in0=rv3[:, :, k:k + SL], scalar=colp.ap[:, 176 + cj * 3 + k:177 + cj * 3 + k],
                                                                      in1=av3, op0=ALU.mult, op1=ALU.add), reads=[rt, colp, at], writes=[at])
                kb.op("act", lambda e: e.activation(out=ag_t.ap[:, 0:TG], in_=ag_t.ap[:, 0:TG], func=AF.Silu, bias=colp.ap[:, 440 + 44 + j:441 + 44 + j]),
                      reads=[ag_t, colp], writes=[ag_t])
                kb.op("dve", lambda e: e.scalar_tensor_tensor(out=act_v[:, j, :], in0=aa_t.ap[:, 0:TG], scalar=colp.ap[:, 440 + j:441 + j],
                                                              in1=ag_t.ap[:, 0:TG], op0=ALU.add, op1=ALU.mult), reads=[aa_t, ag_t, colp], writes=[actT])
            pipelined3(44, up_load, up_compute)
            kb.pop()
            kb.push()
            alloc_pools(4, 3)
            def dn_load(cg):
                wts = []
                for qq in range(2):
                    wb = wbuf()
                    wv_ = wb.ap[:, 0:22 * 128].rearrange("p (c n) -> p c n", c=22)
                    srcv_ = wdn_d[l, qq * 2816:(qq + 1) * 2816, cg * 128:(cg + 1) * 128].rearrange("(c p) n -> p c n", p=128)
                    stage_cast(wb, wv_[:, 0:11, :], srcv_[:, 0:11, :])
                    stage_cast(wb, wv_[:, 11:22, :], srcv_[:, 11:22, :])
                    wts.append((wb, wv_))
                return wts

            def dn_compute(cg, wts):
                for ti in range(TG // 128):
                    pb = pbank()
                    for k in range(44):
                        wb, wv_ = wts[k // 22]
                        kb.op("pe", lambda e: e.matmul(pb.ap[:, 0:128], lhsT=act_v[:, k, ti * 128:(ti + 1) * 128], rhs=wv_[:, k % 22, :],
                                                       start=(k == 0), stop=(k == 43)), reads=[actT, wb], writes=[pb])
                    kb.op("dve", lambda e: e.tensor_tensor(out=ff_v[:, ti, cg * 128:(cg + 1) * 128], in0=pb.ap[:, 0:128], in1=growA.ap[:, cg * 128:(cg + 1) * 128],
                                                           op=ALU.mult), reads=[pb, growA], writes=[bigB])
            pipelined(16, dn_load, dn_compute)
            for ti in range(TG // 128):
                r0 = g0 + ti * 128
                kb.dma(dq(), xres.ap[:], x1_d[r0:r0 + 128, :], reads=[x1_T], writes=[xres])
                kb.op("dve", lambda e: e.scalar_tensor_tensor(out=ff_v[:, ti, :], in0=xres.ap[:], scalar=ALPHA, in1=ff_v[:, ti, :], op0=ALU.mult, op1=ALU.add),
                      reads=[xres, bigB], writes=[bigB])
                layer_norm_rows(bigB, ff_v[:, ti, :], D, lngA, lngA.ap[:], lnbA, lnbA.ap[:], xres)
                kb.dma(dq(), xout_d[r0:r0 + 128, :], xres.ap[:], reads=[xres], writes=[xout_T])
            kb.pop()
        kb.pop()

    try:
        ckpt("mod")
        if "P" in JOBS:
            run_job(0, xp_d, T(xp_d), x1p_d, dT["x1p"], x2p_d, dT["x2p"], 512, 2, 256, 0, False)
        if "S" in JOBS:
            run_job(0, xs_d, T(xs_d), x1s_d, dT["x1s"], x2s_d, dT["x2s"], 2048, 1, 2048, 1, True)
        ckpt("L0")
        if "P" in JOBS:
            run_job(1, x2p_d, dT["x2p"], x1p_d, dT["x1p"], yp_d, dT["yp"], 512, 2, 256, 0, False)
        if "S" in JOBS:
            run_job(1, x2s_d, dT["x2s"], x1s_d, dT["x1s"], ys_d, dT["ys"], 2048, 1, 2048, 1, True)
    except StopBuild:
        if STOP in ("L0", "C") and "S" in JOBS:
            dump("x2s", dT["x2s"], x2s_d[:, :])
            dump("x1s", dT["x1s"], x1s_d[:, :])
        if STOP in ("B1", "B2", "B3"):
            nck = {"B1": 8, "B2": 12, "B3": 16}[STOP]
            dump("ccs", dT["ccs"], ccs_d[:, 0:nck, :], BF16)
    kb.finish()
    return kb.nc


JOBS = "PS"


def _na_bias_table(rpb):
    L = rpb.shape[0]
    a = np.arange(2)[:, None, None, None, None]
    kc = np.arange(64)[None, :, None, None, None]
    d = np.arange(-4, 5)[None, None, :, None, None]
    b = np.arange(2)[None, None, None, :, None]
    qc = np.arange(64)[None, None, None, None, :]
    dr = 2 * d + a - b
    cstart = np.clip(qc - 8, 0, 48)
    ok = (kc >= cstart) & (kc < cstart + 16) & (np.abs(dr) <= 7)
    ri = np.clip(dr + 7, 0, 14) + 0 * kc + 0 * qc
    ci = np.clip(kc - qc + 15, 0, 30) + 0 * dr
    ok = np.broadcast_to(ok, ri.shape)
    out = np.full((L, 16) + ri.shape, NEG, np.float32)
    g = rpb[:, :, ri, ci]
    out[:, :, ok] = g[:, :, ok]
    return np.ascontiguousarray(out.reshape(L, 16, 128, 9 * 128))


def _prep(inputs):
    f = lambda k: np.ascontiguousarray(np.asarray(inputs[k], dtype=np.float32))
    L = 2
    colp = np.zeros((L, 128, 540), np.float32)
    cw = f("conv_w")
    colp[:, :, 0:124] = cw.reshape(L, 31, 4, 128).transpose(0, 3, 2, 1).reshape(L, 128, 124)
    colp[:, :, 124:128] = f("conv_b").reshape(L, 4, 128).transpose(0, 2, 1)
    colp[:, :, 128:168] = f("ssm_conv_w").reshape(L, 5, 8, 128).transpose(0, 3, 2, 1).reshape(L, 128, 40)
    colp[:, :, 168:176] = f("ssm_conv_b").reshape(L, 8, 128).transpose(0, 2, 1)
    colp[:, :, 176:440] = f("ffn_conv_w").reshape(L, 3, 88, 128).transpose(0, 3, 2, 1).reshape(L, 128, 264)
    colp[:, :, 440:528] = f("ffn_conv_b").reshape(L, 88, 128).transpose(0, 2, 1)
    rowp = np.zeros((L, 10272), np.float32)
    rowp[:, 0:2048] = f("ln1_g"); rowp[:, 2048:4096] = f("ln1_b")
    rowp[:, 4096:6144] = f("ln2_g"); rowp[:, 6144:8192] = f("ln2_b")
    rowp[:, 8192:8704] = f("conv_ln_g"); rowp[:, 8704:9216] = f("conv_ln_b")
    rowp[:, 9216:9728] = f("ssm_norm_g")
    rowp[:, 9728:9744] = f("ssm_a_log").reshape(L, 16)
    rowp[:, 9744:9760] = f("ssm_dt_bias").reshape(L, 16)
    rowp[:, 9760:10272] = np.repeat(f("ssm_d"), 64, axis=1)
    return colp, rowp, _na_bias_table(f("rpb"))


def _run(inputs, ncores=8):
    f = lambda k: np.ascontiguousarray(np.asarray(inputs[k], dtype=np.float32))
    colp, rowp, ww = _prep(inputs)
    shared = {"w_mod": f("w_mod"), "b_mod": f("b_mod"), "w_in": f("w_in"), "w_out": f("w_out"), "w_up": f("w_up"),
              "w_down": f("w_down"), "ww": ww, "colp": colp, "rowp": rowp}
    xs, xp, ck, cv, st, c, cctx = f("x_sample"), f("x_prompt"), f("cache_k"), f("cache_v"), f("state_ssm"), f("c"), f("c_ctx")
    in_maps = []
    for core in range(ncores):
        b = core // 2
        m = dict(shared)
        m["xs"] = xs[b]
        m["xp"] = np.ascontiguousarray(xp[2 * core:2 * core + 2].reshape(512, 2048))
        m["ck"] = np.ascontiguousarray(ck[b].reshape(2, 512, 1024))
        m["cv"] = np.ascontiguousarray(cv[b].reshape(2, 512, 1024))
        m["st"] = st[b]
        m["cvec"] = np.ascontiguousarray(np.stack([cctx, c[b]], axis=0))
        in_maps.append(m)
    nc = build()
    res = run_bass_kernel_spmd(nc, in_maps, core_ids=list(range(ncores)))
    return res.results


def kernel(**inputs):
    R = _run(inputs, 8)
    y_prompt = np.concatenate([R[i]["yp"].reshape(2, 256, 2048) for i in range(8)], axis=0)
    y_sample = np.stack([R[2 * b]["ys"] for b in range(4)], axis=0)
    nk = np.concatenate([R[i]["nk"].reshape(2, 2, 256, 16, 64) for i in range(8)], axis=0)
    nv = np.concatenate([R[i]["nv"].reshape(2, 2, 256, 16, 64) for i in range(8)], axis=0)
    ns = np.concatenate([R[i]["ns"] for i in range(8)], axis=0)
    return (y_prompt.astype(np.float32), y_sample.astype(np.float32), nk.astype(np.float32), nv.astype(np.float32), ns.astype(np.float32))
```
